# Optimizing a Trainium2 kernel written in Bass

```python
import math
import jax, jax.numpy as jnp
from jax import lax
import numpy as np

D_MODEL = 1024
BATCH = 16
SEQ = 4096
DEPTH = 4
DEC_BATCH = 32
DEC_SEQ = 2048
PAST_LEN = 128

N_MIXERS = 2
N_ATTN_LAYERS = (DEPTH + N_MIXERS - 1) // N_MIXERS
N_CONV_LAYERS = DEPTH // N_MIXERS
DILATED_GROUPS = ((128, 1), (512, 4), (2048, 16))
N_GROUPS = len(DILATED_GROUPS)
HEADS_PER_GROUP = 8
HEAD_DIM = 128
N_HEADS_TOTAL = N_GROUPS * HEADS_PER_GROUP
ATTN_OUT_WIDTH = HEADS_PER_GROUP * HEAD_DIM
QKV_WIDTH = 3 * N_HEADS_TOTAL * HEAD_DIM
REL_BUCKETS = 32
REL_MAX_DIST = 1024
CONV_WIDTH = 31
CONV_PAD = (CONV_WIDTH - 1) // 2
D_FF = int(math.ceil(8 * D_MODEL / 3 / 256) * 256)
NEG_INF = -1e30
EPS = 1e-6

kernel_name = "hybrid_dilated_attn_conformer_encoder"


def _rmsnorm(x, g):
    xf = x.astype(jnp.float32)
    y = xf * lax.rsqrt(jnp.mean(xf * xf, axis=-1, keepdims=True) + EPS)
    return (y * g.astype(jnp.float32)).astype(x.dtype)


def _layernorm(x, g, b):
    xf = x.astype(jnp.float32)
    mu = jnp.mean(xf, axis=-1, keepdims=True)
    xc = xf - mu
    y = xc * lax.rsqrt(jnp.mean(xc * xc, axis=-1, keepdims=True) + EPS)
    return (y * g.astype(jnp.float32) + b.astype(jnp.float32)).astype(x.dtype)


def _t5_bucket(rel):
    half = REL_BUCKETS // 2
    max_exact = half // 2
    ret = jnp.where(rel > 0, half, 0)
    n = jnp.abs(rel)
    nf = jnp.maximum(n, 1).astype(jnp.float32)
    large = max_exact + (jnp.log(nf / max_exact) / math.log(REL_MAX_DIST / max_exact)
                         * (half - max_exact)).astype(jnp.int32)
    large = jnp.minimum(large, half - 1)
    return ret + jnp.where(n < max_exact, n, large)


def _dilated_group_attention(q, k, v, dil, half, rel_table_g):
    B, S, H, Dh = q.shape
    L = S // dil
    nb = -(-L // half)
    Lp = nb * half

    def to_res(t):
        return t.reshape(B, L, dil, H, Dh).transpose(0, 2, 1, 3, 4)

    qr = jnp.pad(to_res(q), ((0, 0), (0, 0), (0, Lp - L), (0, 0), (0, 0))).reshape(B, dil, nb, half, H, Dh)
    kv_pad = ((0, 0), (0, 0), (half, Lp - L + half), (0, 0), (0, 0))
    kr = jnp.pad(to_res(k), kv_pad).reshape(B, dil, nb + 2, half, H, Dh)
    vr = jnp.pad(to_res(v), kv_pad).reshape(B, dil, nb + 2, half, H, Dh)

    scores = jnp.concatenate(
        [jnp.einsum('brnqhd,brnkhd->brhnqk', qr, kr[:, :, j:j + nb]) for j in range(3)],
        axis=-1).astype(jnp.float32)

    qq = jnp.arange(half)[:, None]
    kk = jnp.arange(3 * half)[None, :]
    rel = kk - half - qq
    key_m = jnp.arange(nb)[:, None, None] * half + qq[None] + rel[None]
    valid = (jnp.abs(rel)[None] <= half) & (key_m >= 0) & (key_m < L)
    bias = rel_table_g.astype(jnp.float32)[_t5_bucket(rel * dil)]
    bias = bias.transpose(2, 0, 1)

    logits = scores + bias[None, None, :, None]
    logits = jnp.where(valid[None, None, None], logits, NEG_INF)
    lse = jax.nn.logsumexp(logits, axis=-1)
    p = jnp.exp(logits - lse[..., None]).astype(v.dtype)
    out = sum(jnp.einsum('brhnqk,brnkhd->brnqhd', p[..., j * half:(j + 1) * half], vr[:, :, j:j + nb])
              for j in range(3))
    out = out.reshape(B, dil, Lp, H, Dh)[:, :, :L].transpose(0, 2, 1, 3, 4).reshape(B, S, H, Dh)
    lse = lse.transpose(0, 1, 3, 4, 2).reshape(B, dil, Lp, H)[:, :, :L]
    lse = lse.transpose(0, 2, 1, 3).reshape(B, S, H)
    return out, lse


def _attention_mixer(h, w_qkv, q_gain, k_gain, w_o, rel_table):
    B, S, _ = h.shape
    qkv = (h @ w_qkv).reshape(B, S, 3, N_GROUPS, HEADS_PER_GROUP, HEAD_DIM)
    q = _rmsnorm(qkv[:, :, 0], q_gain) * (HEAD_DIM ** -0.5)
    k = _rmsnorm(qkv[:, :, 1], k_gain)
    v = qkv[:, :, 2]
    outs, lses = [], []
    for g, (win, dil) in enumerate(DILATED_GROUPS):
        o, l = _dilated_group_attention(
            q[:, :, g], k[:, :, g], v[:, :, g], dil, win // (2 * dil),
            rel_table[:, g * HEADS_PER_GROUP:(g + 1) * HEADS_PER_GROUP])
        outs.append(o)
        lses.append(l)
    wts = jax.nn.softmax(jnp.stack(lses), axis=0)
    o = jnp.einsum('gbsh,gbshd->bshd', wts, jnp.stack(outs).astype(jnp.float32)).astype(h.dtype)
    return o.reshape(B, S, ATTN_OUT_WIDTH) @ w_o


def _conv_mixer(h, w_pw1, b_pw1, w_dw, b_dw, ln_g, ln_b, w_pw2, b_pw2):
    a, gt = jnp.split(h @ w_pw1 + b_pw1, 2, axis=-1)
    u = a * jax.nn.sigmoid(gt)
    u = lax.conv_general_dilated(u, w_dw[:, None, :], (1,), [(CONV_PAD, CONV_PAD)],
                                 dimension_numbers=('NWC', 'WIO', 'NWC'),
                                 feature_group_count=D_MODEL) + b_dw
    u = jax.nn.silu(_layernorm(u, ln_g, ln_b))
    return u @ w_pw2 + b_pw2


def _swiglu(h, w_in, w_out):
    gate, up = jnp.split(h @ w_in, 2, axis=-1)
    return (jax.nn.silu(gate) * up) @ w_out


def _trunk(x, c, rel_bias_table, norm1_g, norm2_g, ada_w, ada_b,
           attn_w_qkv, attn_q_gain, attn_k_gain, attn_w_o,
           conv_w_pw1, conv_b_pw1, conv_w_dw, conv_b_dw, conv_ln_g, conv_ln_b,
           conv_w_pw2, conv_b_pw2, ffn_w_in, ffn_w_out):
    c_act = jax.nn.silu(c)
    for i in range(DEPTH):
        ada = c_act @ ada_w[i] + ada_b[i]
        sh1, sc1, g1, sh2, sc2, g2 = [t[:, None, :] for t in jnp.split(ada, 6, axis=-1)]
        h = _rmsnorm(x, norm1_g[i]) * (1 + sc1) + sh1
        j = i // N_MIXERS
        if i % N_MIXERS == 0:
            y = _attention_mixer(h, attn_w_qkv[j], attn_q_gain[j], attn_k_gain[j],
                                 attn_w_o[j], rel_bias_table)
        else:
            y = _conv_mixer(h, conv_w_pw1[j], conv_b_pw1[j], conv_w_dw[j], conv_b_dw[j],
                            conv_ln_g[j], conv_ln_b[j], conv_w_pw2[j], conv_b_pw2[j])
        x = x + g1 * y
        h = _rmsnorm(x, norm2_g[i]) * (1 + sc2) + sh2
        x = x + g2 * _swiglu(h, ffn_w_in[i], ffn_w_out[i])
    return x


def setup_inputs(seed: int = 0) -> dict:
    key = jax.random.key(seed)
    ks = jax.random.split(key, 24)
    f32 = jnp.float32
    nrm = lambda k, shape, s: jax.random.normal(k, shape, f32) * s
    D = D_MODEL
    return {
        "x_prompt": nrm(ks[0], (BATCH, SEQ, D), 1.0),
        "x_sample": nrm(ks[1], (DEC_BATCH, DEC_SEQ, D), 1.0),
        "c_prompt": nrm(ks[2], (BATCH, D), 1.0),
        "c_sample": nrm(ks[3], (DEC_BATCH, D), 1.0),
        "rel_bias_table": nrm(ks[4], (REL_BUCKETS, N_HEADS_TOTAL), 0.5),
        "norm1_g": 1.0 + nrm(ks[5], (DEPTH, D), 0.02),
        "norm2_g": 1.0 + nrm(ks[6], (DEPTH, D), 0.02),
        "ada_w": nrm(ks[7], (DEPTH, D, 6 * D), 0.3 * D ** -0.5),
        "ada_b": nrm(ks[8], (DEPTH, 6 * D), 0.01),
        "attn_w_qkv": nrm(ks[9], (N_ATTN_LAYERS, D, QKV_WIDTH), D ** -0.5),
        "attn_q_gain": 1.0 + nrm(ks[10], (N_ATTN_LAYERS, HEAD_DIM), 0.02),
        "attn_k_gain": 1.0 + nrm(ks[11], (N_ATTN_LAYERS, HEAD_DIM), 0.02),
        "attn_w_o": nrm(ks[12], (N_ATTN_LAYERS, ATTN_OUT_WIDTH, D), ATTN_OUT_WIDTH ** -0.5),
        "conv_w_pw1": nrm(ks[13], (N_CONV_LAYERS, D, 2 * D), D ** -0.5),
        "conv_b_pw1": nrm(ks[14], (N_CONV_LAYERS, 2 * D), 0.01),
        "conv_w_dw": nrm(ks[15], (N_CONV_LAYERS, CONV_WIDTH, D), CONV_WIDTH ** -0.5),
        "conv_b_dw": nrm(ks[16], (N_CONV_LAYERS, D), 0.01),
        "conv_ln_g": 1.0 + nrm(ks[17], (N_CONV_LAYERS, D), 0.02),
        "conv_ln_b": nrm(ks[18], (N_CONV_LAYERS, D), 0.01),
        "conv_w_pw2": nrm(ks[19], (N_CONV_LAYERS, D, D), D ** -0.5),
        "conv_b_pw2": nrm(ks[20], (N_CONV_LAYERS, D), 0.01),
        "ffn_w_in": nrm(ks[21], (DEPTH, D, 2 * D_FF), D ** -0.5),
        "ffn_w_out": nrm(ks[22], (DEPTH, D_FF, D), D_FF ** -0.5),
    }


def reference(x_prompt, x_sample, c_prompt, c_sample, rel_bias_table, norm1_g, norm2_g,
              ada_w, ada_b, attn_w_qkv, attn_q_gain, attn_k_gain, attn_w_o,
              conv_w_pw1, conv_b_pw1, conv_w_dw, conv_b_dw, conv_ln_g, conv_ln_b,
              conv_w_pw2, conv_b_pw2, ffn_w_in, ffn_w_out):
    y_prompt = _trunk(x_prompt, c_prompt, rel_bias_table, norm1_g, norm2_g, ada_w, ada_b,
                      attn_w_qkv, attn_q_gain, attn_k_gain, attn_w_o,
                      conv_w_pw1, conv_b_pw1, conv_w_dw, conv_b_dw, conv_ln_g, conv_ln_b,
                      conv_w_pw2, conv_b_pw2, ffn_w_in, ffn_w_out)
    y_sample = _trunk(x_sample, c_sample, rel_bias_table, norm1_g, norm2_g, ada_w, ada_b,
                      attn_w_qkv, attn_q_gain, attn_k_gain, attn_w_o,
                      conv_w_pw1, conv_b_pw1, conv_w_dw, conv_b_dw, conv_ln_g, conv_ln_b,
                      conv_w_pw2, conv_b_pw2, ffn_w_in, ffn_w_out)
    return (y_prompt, y_sample)
```

```python
import math
import numpy as np
import concourse.bass as bass
import concourse.mybir as mybir
from concourse.bass_utils import run_bass_kernel_spmd

F32 = mybir.dt.float32
BF16 = mybir.dt.bfloat16
AF = mybir.ActivationFunctionType
ALU = mybir.AluOpType

D = 1024
DFF = 2816
NL = 4
QKVW = 9216
EPS = 1e-6
GROUPS = ((128, 1), (512, 4), (2048, 16))
CONVW = 31
TT = 512


class Buf:
    __slots__ = ("w", "r")

    def __init__(self):
        self.w = None
        self.r = []


def bufs(n):
    return [Buf() for _ in range(n)]


class Eng:
    def __init__(self, name, sem, same_sync):
        self.name = name
        self.sem = sem
        self.cnt = 0
        self.seen = {}
        self.q = []
        self.same_sync = same_sync


class FW:
    def __init__(self, nc):
        self.nc = nc
        self.sems = []
        self.E = {}
        for name, same in (("pe", False), ("act", True), ("dve", True), ("pool", True), ("sp", False)):
            self.E[name] = Eng(name, self._sem("e_" + name), same)
        self.dsem = {}

    def _sem(self, name):
        cm = self.nc.semaphore(name)
        s = cm.__enter__()
        self.sems.append(cm)
        return s

    def _waits(self, eng, deps):
        need = {}
        for (s, v) in deps:
            if s is eng.sem and not eng.same_sync:
                continue
            k = id(s)
            if eng.seen.get(k, 0) >= v:
                continue
            if k not in need or need[k][1] < v:
                need[k] = (s, v)
        for k, (s, v) in need.items():
            eng.seen[k] = v
        return list(need.values())

    @staticmethod
    def _deps(reads, writes):
        deps = []
        for b in reads:
            if b.w is not None:
                deps.append(b.w)
        for b in writes:
            if b.w is not None:
                deps.append(b.w)
            deps.extend(b.r)
        return deps

    def op(self, en, fn, reads=(), writes=(), inc=True, relaxed=False):
        eng = self.E[en]
        deps = self._deps(reads, writes)
        if relaxed:
            deps = [d for d in deps if d[0] is not eng.sem]
        waits = self._waits(eng, deps)
        idx = eng.cnt + 1
        if inc:
            eng.cnt = idx
        tok = (eng.sem, idx)
        eng.q.append((waits, fn, eng.sem if inc else None, 1))
        for b in reads:
            b.r.append(tok)
        for b in writes:
            b.w = tok
            b.r = []
        return tok

    def dma(self, qn, slot, out, in_, reads=(), writes=()):
        eng = self.E[qn]
        if slot not in self.dsem:
            self.dsem[slot] = [self._sem("d_" + slot), 0]
        ds = self.dsem[slot]
        waits = self._waits(eng, self._deps(reads, writes))
        ds[1] += 16
        tok = (ds[0], ds[1])
        eng.q.append((waits, (lambda e, o=out, i=in_: e.dma_start(out=o, in_=i)), ds[0], 16))
        for b in reads:
            b.r.append(tok)
        for b in writes:
            b.w = tok
            b.r = []
        return tok

    def barrier(self):
        toks = []
        for e in self.E.values():
            if e.cnt > 0:
                toks.append((e.sem, e.cnt))
        for s, c in self.dsem.values():
            if c > 0:
                toks.append((s, c))
        for e in self.E.values():
            waits = []
            for (s, v) in toks:
                if s is e.sem:
                    continue
                k = id(s)
                if e.seen.get(k, 0) >= v:
                    continue
                e.seen[k] = v
                waits.append((s, v))
            if waits:
                e.q.append((waits, None, None, 0))

    def run(self):
        nc = self.nc
        with nc.Block() as block:
            def mk(en):
                def body(eng):
                    for (waits, fn, sem, inc) in self.E[en].q:
                        for (s, v) in waits:
                            eng.wait_ge(s, v)
                        if fn is not None:
                            ins = fn(eng)
                            if sem is not None:
                                ins.then_inc(sem, inc)
                return body
            block.tensor(mk("pe"))
            block.scalar(mk("act"))
            block.vector(mk("dve"))
            block.gpsimd(mk("pool"))
            block.sync(mk("sp"))


class Ring:
    def __init__(self, items):
        self.items = items
        self.i = 0

    def next(self):
        it = self.items[self.i % len(self.items)]
        self.i += 1
        return it


class SBAlloc:
    def __init__(self, nc, words):
        self.t = nc.alloc_sbuf_tensor("sb_all", [128, words], F32)
        self.words = words
        self.off = 0
        self.stack = []
        self.peak = 0
        self.on_pop = None

    def push(self):
        self.stack.append(self.off)

    def pop(self):
        self.off = self.stack.pop()
        if self.on_pop is not None:
            self.on_pop()

    def f32(self, n):
        ap = self.t[:, self.off:self.off + n]
        self.off += n
        self.peak = max(self.peak, self.off)
        assert self.off <= self.words, ("SBUF overflow", self.off, self.words)
        return ap

    def bf16(self, n):
        assert n % 2 == 0
        w = n // 2
        ap = self.t[:, self.off:self.off + w].bitcast(BF16)
        self.off += w
        self.peak = max(self.peak, self.off)
        assert self.off <= self.words, ("SBUF overflow", self.off, self.words)
        return ap


def v3(ap, b):
    return ap.rearrange("p (a b) -> p a b", b=b)


def _t5_bucket_np(rel):
    half = 16
    max_exact = 8
    ret = np.where(rel > 0, half, 0)
    n = np.abs(rel)
    nf = np.maximum(n, 1).astype(np.float32)
    large = max_exact + (np.log(nf / np.float32(max_exact)) / np.float32(math.log(1024 / max_exact))
                         * np.float32(half - max_exact)).astype(np.int32)
    large = np.minimum(large, half - 1)
    return ret + np.where(n < max_exact, n, large)


def make_onehot():
    oh = np.zeros((3, 32, 384), np.float32)
    j = np.arange(384)
    rel = 191 - j
    valid = np.abs(rel) <= 64
    for g, (win, dil) in enumerate(GROUPS):
        b = _t5_bucket_np(rel * dil)
        for jj in range(384):
            if valid[jj]:
                oh[g, b[jj], jj] = 1.0
    return oh


def build_program(n_p, n_s, s_p=4096, s_s=2048, debug=None):
    nc = bass.Bass("TRN2", target_bir_lowering=False)
    fw = FW(nc)
    NSEQ = n_p + n_s
    seqs = [("p", i, s_p) for i in range(n_p)] + [("s", i, s_s) for i in range(n_s)]
    SMAX = max(s for _, _, s in seqs)

    def ext_in(name, shape):
        return nc.dram_tensor(name, list(shape), F32, kind="ExternalInput").ap()

    xin_p = ext_in("xp", (n_p, s_p, D)) if n_p else None
    xin_s = ext_in("xs", (n_s, s_s, D)) if n_s else None
    c_p = ext_in("cp", (n_p, D)) if n_p else None
    c_s = ext_in("cs", (n_s, D)) if n_s else None
    rel_tab = ext_in("rel_bias_table", (32, 24))
    norm1_g = ext_in("norm1_g", (NL, D))
    norm2_g = ext_in("norm2_g", (NL, D))
    ada_w = ext_in("ada_w", (NL, D, 6 * D))
    ada_b = ext_in("ada_b", (NL, 6 * D))
    w_qkv = ext_in("attn_w_qkv", (2, D, QKVW))
    q_gain = ext_in("attn_q_gain", (2, 128))
    k_gain = ext_in("attn_k_gain", (2, 128))
    w_o = ext_in("attn_w_o", (2, D, D))
    w_pw1 = ext_in("conv_w_pw1", (2, D, 2 * D))
    b_pw1 = ext_in("conv_b_pw1", (2, 2 * D))
    w_dw = ext_in("conv_w_dw", (2, CONVW, D))
    b_dw = ext_in("conv_b_dw", (2, D))
    ln_g = ext_in("conv_ln_g", (2, D))
    ln_b = ext_in("conv_ln_b", (2, D))
    w_pw2 = ext_in("conv_w_pw2", (2, D, D))
    b_pw2 = ext_in("conv_b_pw2", (2, D))
    w_in = ext_in("ffn_w_in", (NL, D, 2 * DFF))
    w_out = ext_in("ffn_w_out", (NL, DFF, D))
    ident_in = ext_in("ident", (128, 128))
    oh_in = ext_in("onehot", (3, 32, 384))

    y_p = nc.dram_tensor("yp", [n_p, s_p, D], F32, kind="ExternalOutput").ap() if n_p else None
    y_s = nc.dram_tensor("ys", [n_s, s_s, D], F32, kind="ExternalOutput").ap() if n_s else None

    def scr(name, shape, dt):
        if debug is not None and name in ("xT_s", "hT_s", "oT_s", "uT_s", "aT_s"):
            return nc.dram_tensor(name, list(shape), dt, kind="ExternalOutput").ap()
        return nc.dram_tensor(name, list(shape), dt).ap()

    wb_qkv = scr("wb_qkv", (2, D, QKVW), BF16)
    wb_o = scr("wb_o", (2, D, D), BF16)
    wb_pw1 = scr("wb_pw1", (2, D, 2 * D), BF16)
    wb_pw2 = scr("wb_pw2", (2, D, D), BF16)
    wb_in = scr("wb_in", (NL, D, 2 * DFF), BF16)
    wb_out = scr("wb_out", (NL, DFF, D), BF16)
    xT_s = scr("xT_s", (128, 8, SMAX), F32)
    hT_s = scr("hT_s", (128, 8, SMAX), BF16)
    oT_s = scr("oT_s", (128, 8, SMAX), BF16)
    uT_s = scr("uT_s", (128, 8, SMAX), BF16)
    aT_s = scr("aT_s", (128, 22, SMAX), BF16)
    dg_s = scr("dg_s", (2, 128, 8 * CONVW * 128), BF16)
    E_s = scr("E_s", (24, 128, 384), BF16)

    sb = SBAlloc(nc, 53000)
    sb.on_pop = fw.barrier
    ps = nc.alloc_psum_tensor("ps_all", [128, 8, 512], F32)
    psB = bufs(8)

    def dbg_dump(name, ap, n, dt, reads):
        if debug is None:
            return
        t = nc.dram_tensor(name, [128, n], dt, kind="ExternalOutput").ap()
        fw.dma("pool", "dbg", t[:, :], ap, reads=reads)

    def mm(out, lhsT, rhs, start, stop, reads, writes, inc):
        fw.op("pe", lambda e: e.matmul(out, lhsT, rhs, start=start, stop=stop), reads, writes, inc)

    def tr(out, in_, idn, reads, writes, inc):
        fw.op("pe", lambda e: e.transpose(out=out, in_=in_, identity=idn), reads, writes, inc)

    def act(out, in_, func, reads, writes, bias=None, scale=None):
        kw = {}
        if bias is not None:
            kw["bias"] = bias
        if scale is not None:
            kw["scale"] = scale
        fw.op("act", lambda e: e.activation(out=out, in_=in_, func=func, **kw), reads, writes)

    def tt(en, out, in0, in1, op, reads, writes, relaxed=False):
        fw.op(en, lambda e: e.tensor_tensor(out=out, in0=in0, in1=in1, op=op), reads, writes, relaxed=relaxed)

    def ts(en, out, in0, s1, s2, op0, op1, reads, writes):
        if s2 is None:
            fw.op(en, lambda e: e.tensor_scalar(out=out, in0=in0, scalar1=s1, scalar2=None, op0=op0), reads, writes)
        else:
            fw.op(en, lambda e: e.tensor_scalar(out=out, in0=in0, scalar1=s1, scalar2=s2, op0=op0, op1=op1), reads, writes)

    def stt(en, out, in0, scalar, in1, op0, op1, reads, writes):
        fw.op(en, lambda e: e.scalar_tensor_tensor(out=out, in0=in0, scalar=scalar, in1=in1, op0=op0, op1=op1), reads, writes)

    def cp(en, out, in_, reads, writes, relaxed=False):
        if en == "act":
            act(out, in_, AF.Copy, reads, writes)
        else:
            fw.op(en, lambda e: e.tensor_copy(out=out, in_=in_), reads, writes, relaxed=relaxed)

    def recip(out, in_, reads, writes):
        fw.op("dve", lambda e: e.reciprocal(out=out, in_=in_), reads, writes)

    def memset(en, ap, val, writes):
        fw.op(en, lambda e: e.memset(ap, val), (), writes)

    ident = sb.f32(128)
    identB = Buf()
    ones_b = sb.bf16(128)
    ones_f = sb.f32(128)
    constB = Buf()
    epsc = sb.f32(2)
    fw.dma("sp", "misc0", ident, ident_in[:, :], writes=[identB])
    memset("dve", ones_b, 1.0, [constB])
    memset("dve", ones_f, 1.0, [constB])
    memset("dve", epsc, EPS, [constB])

    entries = []
    if n_p:
        entries.append(("c_p", c_p.rearrange("s (c p) -> (s c) p", p=128)))
    if n_s:
        entries.append(("c_s", c_s.rearrange("s (c p) -> (s c) p", p=128)))
    entries += [
        ("ada_b", ada_b.rearrange("l (c p) -> (l c) p", p=128)),
        ("n1g", norm1_g.rearrange("l (c p) -> (l c) p", p=128)),
        ("n2g", norm2_g.rearrange("l (c p) -> (l c) p", p=128)),
        ("bpw1", b_pw1.rearrange("l (c p) -> (l c) p", p=128)),
        ("wdw", w_dw.rearrange("l t (c p) -> (l t c) p", p=128)),
        ("bdw", b_dw.rearrange("l (c p) -> (l c) p", p=128)),
        ("lng", ln_g.rearrange("l (c p) -> (l c) p", p=128)),
        ("lnb", ln_b.rearrange("l (c p) -> (l c) p", p=128)),
        ("bpw2", b_pw2.rearrange("l (c p) -> (l c) p", p=128)),
        ("qg", q_gain),
        ("kg", k_gain),
    ]
    ncols = sum(a.shape[0] for _, a in entries)
    coltab = sb.f32(ncols)
    coltabB = Buf()
    colbase = {}
    sb.push()
    stage = [(sb.f32(128), Buf()) for _ in range(2)]
    stg = Ring(stage)
    pbank = Ring([(ps[:, b, :], psB[b]) for b in range(2)])
    base = 0
    for name, ap in entries:
        colbase[name] = base
        rows = ap.shape[0]
        r0 = 0
        while r0 < rows:
            n = min(128, rows - r0)
            st, stB = stg.next()
            pb, pbB = pbank.next()
            fw.dma("sp", "stage%d" % (stg.i % 2), st[0:n, :], ap[r0:r0 + n, :], writes=[stB])
            tr(pb[:, 0:n], st[0:n, :], ident[0:n, 0:n], [stB, identB], [pbB], True)
            cp("act", coltab[:, base + r0:base + r0 + n], pb[:, 0:n], [pbB], [coltabB])
            r0 += n
        base += rows
    sb.pop()

    def col(name, idx, n=1):
        b0 = colbase[name] + idx
        return coltab[:, b0:b0 + n]

    cbase = colbase["c_p"] if n_p else colbase["c_s"]

    adaT = sb.f32(NL * NSEQ * 48)
    adaB = Buf()
    sb.push()
    cact = sb.f32(NSEQ * 8)
    cactB = Buf()
    act(cact, coltab[:, cbase:cbase + NSEQ * 8], AF.Silu, [coltabB], [cactB])
    awt = [(v3(sb.f32(8 * 512), 512), Buf()) for _ in range(2)]
    awr = Ring(awt)
    for l in range(NL):
        pa, paB = ps[:, 2 + (l % 2), :], psB[2 + (l % 2)]
        for cb in range(12):
            wt, wtB = awr.next()
            fw.dma("sp", "adaw%d" % (awr.i % 2), wt, ada_w[l, :, cb * 512:(cb + 1) * 512].rearrange("(kc p) c -> p kc c", p=128),
                   writes=[wtB])
            for jj in range(4):
                j = cb * 4 + jj
                for kc in range(8):
                    mm(pa[:, j * NSEQ:(j + 1) * NSEQ], wt[:, kc, jj * 128:(jj + 1) * 128],
                       cact[:, kc:kc + 8 * (NSEQ - 1) + 1:8], kc == 0, kc == 7,
                       [wtB, cactB], [paB], (jj == 3 and kc == 7))
        for s in range(NSEQ):
            o0 = (l * NSEQ + s) * 48
            tt("dve", adaT[:, o0:o0 + 48], pa[:, s:s + NSEQ * 47 + 1:NSEQ], col("ada_b", l * 48, 48), ALU.add,
               [paB, coltabB], [adaB])
    sb.pop()

    def ada(l, s, k):
        o0 = (l * NSEQ + s) * 48 + k * 8
        return adaT[:, o0:o0 + 8]

    sb.push()
    tab = sb.f32(24)
    tabB = Buf()
    fw.dma("sp", "misc1", tab[0:32, :], rel_tab[:, :], writes=[tabB])
    act(tab[0:32, :], tab[0:32, :], AF.Exp, [tabB], [tabB])
    oh = v3(sb.f32(3 * 384), 384)
    ohB = Buf()
    fw.dma("sp", "misc2", oh[0:32, :, :], oh_in.rearrange("g b j -> b g j"), writes=[ohB])
    ebs = [(sb.f32(128), Buf()) for _ in range(2)]
    ebr = Ring(ebs)
    dts = [(sb.bf16(384), Buf()) for _ in range(2)]
    dtr = Ring(dts)
    pbank = Ring([(ps[:, b, :], psB[b]) for b in range(4, 6)])
    for gh in range(24):
        g = gh // 8
        eb, ebB = ebr.next()
        ts("dve", eb[0:32, :], ones_f[0:32, :], tab[0:32, gh:gh + 1], None, ALU.mult, None, [constB, tabB], [ebB])
        pb, pbB = pbank.next()
        mm(pb[:, 0:384], eb[0:32, :], oh[0:32, g, :], True, True, [ebB, ohB], [pbB], True)
        dt_, dtB = dtr.next()
        cp("act", dt_, pb[:, 0:384], [pbB], [dtB])
        fw.dma("pool", "est%d" % (gh % 2), E_s[gh], dt_, reads=[dtB])
    sb.pop()

    sb.push()
    FW_ = 2048
    cin = [(sb.f32(FW_), Buf()) for _ in range(3)]
    cout = [(sb.bf16(FW_), Buf()) for _ in range(3)]
    cinr, coutr = Ring(cin), Ring(cout)
    k = 0
    for src, dst in ((w_qkv, wb_qkv), (w_o, wb_o), (w_pw1, wb_pw1), (w_pw2, wb_pw2), (w_in, wb_in), (w_out, wb_out)):
        sf = src.rearrange("l r c -> (l r c)").rearrange("(n p f) -> n p f", p=128, f=FW_)
        df = dst.rearrange("l r c -> (l r c)").rearrange("(n p f) -> n p f", p=128, f=FW_)
        for n in range(sf.shape[0]):
            ci, ciB = cinr.next()
            co, coB = coutr.next()
            fw.dma("sp", "cvi%d" % (k % 3), ci, sf[n], writes=[ciB])
            cp(("act", "dve", "pool")[k % 3], co, ci, [ciB], [coB])
            fw.dma("pool", "cvo%d" % (k % 3), df[n], co, reads=[coB])
            k += 1
    sb.pop()

    sb.push()
    identb = sb.bf16(128)
    identbB = Buf()
    cp("dve", identb, ident, [identB], [identbB])
    dgt = [(sb.bf16(CONVW * 128), Buf()) for _ in range(2)]
    dgr = Ring(dgt)
    for j in range(2):
        for c in range(8):
            dgb, dgB = dgr.next()
            for t in range(CONVW):
                ts("dve", dgb[:, t * 128:(t + 1) * 128], identb, col("wdw", (j * CONVW + t) * 8 + c), None,
                   ALU.mult, None, [identbB, coltabB], [dgB])
            fw.dma("pool", "dgst%d" % (c % 2), dg_s[j, :, c * CONVW * 128:(c + 1) * CONVW * 128], dgb, reads=[dgB])
    sb.pop()

    def norm_mod(xt, xtB, ht, htB, gs, sh, modB, tmp, sq_ring, ss_bank, r_ap, rB):
        ssp, sspB = ss_bank
        for c in range(8):
            sq, sqB = sq_ring.next()
            act(sq, xt[:, c, :], AF.Square, [xtB[c]], [sqB])
            mm(ssp, ones_b, sq, c == 0, c == 7, [sqB, constB], [sspB], True)
        act(r_ap, ssp, AF.Sqrt, [sspB, constB], [rB], bias=epsc[:, 0:1], scale=1.0 / D)
        recip(r_ap, r_ap, [rB], [rB])
        for c in range(8):
            tp, tpB = tmp.next()
            stt("dve", tp, xt[:, c, :], gs[:, c:c + 1], r_ap, ALU.mult, ALU.mult, [xtB[c], modB, rB], [tpB])
            act(ht[:, c, :], tp, AF.Identity, [tpB, modB], [htB[c]], bias=sh[:, c:c + 1], scale=1.0)

    def mod_gs(dst, sc, gcol, modB):
        stt("dve", dst, sc, 1.0, gcol, ALU.add, ALU.mult, [adaB, coltabB], [modB])

    def xrows(si, t0, n):
        kind, i, S = seqs[si]
        src = xin_p if kind == "p" else xin_s
        return src[i, t0:t0 + n, :]

    def yrows(si, t0, n):
        kind, i, S = seqs[si]
        dst = y_p if kind == "p" else y_s
        return dst[i, t0:t0 + n, :]

    def phase_in(si):
        S = seqs[si][2]
        sb.push()
        xin = [(v3(sb.f32(4 * D), D), Buf()) for _ in range(2)]
        xinr = Ring(xin)
        xt = v3(sb.f32(8 * TT), TT)
        xtB = bufs(8)
        ht = v3(sb.bf16(8 * TT), TT)
        htB = bufs(8)
        gs = sb.f32(8)
        modB = Buf()
        mod_gs(gs, ada(0, si, 1), col("n1g", 0, 8), modB)
        sh = ada(0, si, 0)
        tmp = Ring([(sb.f32(TT), Buf()) for _ in range(2)])
        sqr = Ring([(sb.bf16(TT), Buf()) for _ in range(2)])
        r_ap, rB = sb.f32(TT), Buf()
        pbank = Ring([(ps[:, b, :], psB[b]) for b in range(0, 4)])
        for ti in range(S // TT):
            t0 = ti * TT
            xi, xiB = xinr.next()
            for s4 in range(4):
                fw.dma("sp", "in%d" % (ti % 2), xi[:, s4, :], xrows(si, t0 + s4 * 128, 128), writes=[xiB])
            for c in range(8):
                pb, pbB = pbank.next()
                for s4 in range(4):
                    tr(pb[:, s4 * 128:(s4 + 1) * 128], xi[:, s4, c * 128:(c + 1) * 128], ident, [xiB, identB], [pbB],
                       s4 == 3)
                cp("act" if c % 2 else "dve", xt[:, c, :], pb, [pbB], [xtB[c]])
            fw.dma("pool", "stx", xT_s[:, :, t0:t0 + TT], xt, reads=xtB)
            norm_mod(xt, xtB, ht, htB, gs, sh, modB, tmp, sqr, (ps[:, 6, :], psB[6]), r_ap, rB)
            fw.dma("pool", "sth", hT_s[:, :, t0:t0 + TT], ht, reads=htB)
        sb.pop()

    def phase_att(si, l):
        S = seqs[si][2]
        j = l // 2
        ntt = S // TT
        sb.push()
        E = v3(sb.bf16(24 * 256), 256)
        EB = Buf()
        for gh in range(24):
            fw.dma("sp", "eld", E[:, gh, :], bass.AP(E_s.tensor, gh * 128 * 384 + 127, [[383, 128], [1, 256]]),
                   writes=[EB])
        hT = v3(sb.bf16(8 * S), S)
        hTall = Buf()
        hTB = [hTall] * ntt
        for ti in range(ntt):
            fw.dma("sp", "hld", hT[:, :, ti * TT:(ti + 1) * TT], hT_s[:, :, ti * TT:(ti + 1) * TT], writes=[hTall])
        wsl = [(sb.bf16(3 * 8 * 128).rearrange("p (t k c) -> p t k c", t=3, k=8), Buf()) for _ in range(2)]
        wr = Ring(wsl)
        qT, kT = sb.bf16(S), sb.bf16(S)
        qkB = [bufs(ntt), bufs(ntt)]
        V = v3(sb.bf16(S), 128)
        nkt = S // 128
        VB = bufs(nkt // 4)
        Oacc, sacc = sb.f32(S), sb.f32(S)
        OB = Buf()
        SB_ = Buf()
        oout = sb.bf16(S)
        ooutB = Buf()
        sqr = Ring([(sb.bf16(TT), Buf()) for _ in range(2)])
        rawr = Ring([(sb.f32(TT), Buf()) for _ in range(2)])
        rr = Ring([(sb.f32(TT), Buf()) for _ in range(2)])
        exr = Ring([(sb.bf16(256), Buf()) for _ in range(3)])
        ptr_ = Ring([(sb.bf16(256), Buf()) for _ in range(3)])
        gq = sb.f32(2)
        gqB = Buf()
        ts("dve", gq[:, 0:1], col("qg", j), 128.0 ** -0.5, None, ALU.mult, None, [coltabB], [gqB])
        ts("dve", gq[:, 1:2], col("kg", j), 1.0, None, ALU.mult, None, [coltabB], [gqB])
        pq = Ring([(ps[:, b, :], psB[b]) for b in (0, 1)])
        pss = (ps[:, 2, :], psB[2])
        pv = (ps[:, 3, :], psB[3])
        pst = Ring([(ps[:, b, 0:256], psB[b]) for b in (4, 5, 3)])
        po = Ring([(ps[:, b, 0:256], psB[b]) for b in (6, 7)])

        for h in range(8):
            for g, (win, dil) in enumerate(GROUPS):
                gh = g * 8 + h
                L = S // dil
                nt = L // 128
                w, wB = wr.next()
                for t in range(3):
                    c0 = t * 3072 + g * 1024 + h * 128
                    fw.dma("sp", "aw%d" % ((h * 3 + g) % 2), w[:, t, :, :],
                           wb_qkv[j, :, c0:c0 + 128].rearrange("(kc p) c -> p kc c", p=128), writes=[wB])
                for ti in range(ntt):
                    for t in range(2):
                        pb, pbB = pq.next()
                        for kc in range(8):
                            mm(pb, w[:, t, kc, :], hT[:, kc, ti * TT:(ti + 1) * TT], kc == 0, kc == 7,
                               [wB, hTB[ti]], [pbB], kc == 7)
                        sq, sqB = sqr.next()
                        raw, rawB = rawr.next()
                        r_, rB_ = rr.next()
                        act(sq, pb, AF.Square, [pbB], [sqB])
                        cp("act", raw, pb, [pbB], [rawB])
                        mm(pss[0], ones_b, sq, True, True, [sqB, constB], [pss[1]], True)
                        act(r_, pss[0], AF.Sqrt, [pss[1], constB], [rB_], bias=epsc[:, 0:1], scale=1.0 / 128.0)
                        recip(r_, r_, [rB_], [rB_])
                        dst = (qT, kT)[t]
                        stt("dve", dst[:, ti * TT:(ti + 1) * TT], raw, gq[:, t:t + 1], r_, ALU.mult, ALU.mult,
                            [rawB, gqB, rB_], [qkB[t][ti]])
                tiles = [(r, i) for r in range(dil) for i in range(nt)]
                for v0 in range(0, nkt, 4):
                    for u in range(4):
                        r, i = tiles[v0 + u]
                        a0 = r + dil * 128 * i
                        for kc in range(8):
                            mm(pv[0][:, u * 128:(u + 1) * 128], hT[:, kc, a0:a0 + dil * 127 + 1:dil], w[:, 2, kc, :],
                               kc == 0, kc == 7, [wB, hTall], [pv[1]], (u == 3 and kc == 7))
                    cp("dve", V[:, v0:v0 + 4, :], v3(pv[0], 128), [pv[1]], [VB[v0 // 4]])

                def qk(n):
                    r, i = tiles[n]
                    qlo = max(0, 128 * i - 64)
                    qhi = min(L, 128 * i + 192)
                    nq = qhi - qlo
                    sp_, spB = pst.next()
                    k0 = r + dil * 128 * i
                    q0 = r + dil * qlo
                    mm(sp_[:, 0:nq], kT[:, k0:k0 + dil * 127 + 1:dil], qT[:, q0:q0 + dil * (nq - 1) + 1:dil],
                       True, True, qkB[0] + qkB[1], [spB], True)
                    return (sp_, spB, qlo, nq)

                pend = {}
                for n in range(min(2, len(tiles))):
                    pend[n] = qk(n)
                prev = None
                for n in range(len(tiles)):
                    if n + 2 < len(tiles):
                        pend[n + 2] = qk(n + 2)
                    r, i = tiles[n]
                    sp_, spB, qlo, nq = pend.pop(n)
                    off = qlo - (128 * i - 64)
                    ex, exB = exr.next()
                    pt, ptB = ptr_.next()
                    act(ex[:, 0:nq], sp_[:, 0:nq], AF.Exp, [spB], [exB])
                    tt("dve", pt[:, 0:nq], ex[:, 0:nq], E[:, gh, off:off + nq], ALU.mult, [exB, EB], [ptB])
                    if i == 0:
                        prev = None
                    cur = (n, pt, ptB, qlo)
                    done = [i] + ([nt] if i == nt - 1 else [])
                    for reg in done:
                        qa, qb = max(0, 128 * reg - 64), min(L, 128 * reg + 64)
                        wd = qb - qa
                        contrib = []
                        if reg >= 1:
                            pn, ppt, pptB, pqlo = prev if reg == i else cur
                            contrib.append((pn, ppt, pptB, qa - pqlo))
                        if reg <= nt - 1:
                            contrib.append((n, pt, ptB, qa - qlo))
                        pr, prB = po.next()
                        for ci, (vn, cpt, cptB, c0) in enumerate(contrib):
                            mm(pr[:, 0:wd], V[:, vn, :], cpt[:, c0:c0 + wd], ci == 0, ci == len(contrib) - 1,
                               [VB[vn // 4], cptB], [prB], False)
                        for ci, (vn, cpt, cptB, c0) in enumerate(contrib):
                            mm(pr[:, 128:128 + wd], ones_b, cpt[:, c0:c0 + wd], ci == 0, ci == len(contrib) - 1,
                               [cptB, constB], [prB], ci == len(contrib) - 1)
                        p0 = r + dil * qa
                        osl = Oacc[:, p0:p0 + dil * (wd - 1) + 1:dil]
                        ssl = sacc[:, p0:p0 + dil * (wd - 1) + 1:dil]
                        if g == 0:
                            cp("dve", osl, pr[:, 0:wd], [prB], [OB], relaxed=True)
                            cp("dve", ssl, pr[:, 128:128 + wd], [prB], [SB_], relaxed=True)
                        else:
                            tt("dve", osl, pr[:, 0:wd], osl, ALU.add, [prB, OB], [OB], relaxed=True)
                            tt("dve", ssl, pr[:, 128:128 + wd], ssl, ALU.add, [prB, SB_], [SB_], relaxed=True)
                    prev = cur
            for c0 in range(0, S, 2048):
                recip(sacc[:, c0:c0 + 2048], sacc[:, c0:c0 + 2048], [SB_], [SB_])
                tt("pool", oout[:, c0:c0 + 2048], Oacc[:, c0:c0 + 2048], sacc[:, c0:c0 + 2048], ALU.mult,
                   [OB, SB_], [ooutB])
            fw.dma("pool", "sto", oT_s[:, h, 0:S], oout, reads=[ooutB])
        if l == 0 and si == 0:
            dbg_dump("d_qT", qT, S, BF16, qkB[0])
            dbg_dump("d_kT", kT, S, BF16, qkB[1])
            dbg_dump("d_V", V.rearrange("p a b -> p (a b)"), S, BF16, VB)
            dbg_dump("d_E", E.rearrange("p a b -> p (a b)"), 24 * 256, BF16, [EB])
            dbg_dump("d_O", Oacc, S, F32, [OB])
            dbg_dump("d_s", sacc, S, F32, [SB_])
            dbg_dump("d_oo", oout, S, BF16, [ooutB])
        sb.pop()

    def phase_ta(si, l):
        S = seqs[si][2]
        j = l // 2
        is_attn = (l % 2 == 0)
        sb.push()
        wproj = v3(sb.bf16(8 * D), D)
        win = v3(sb.bf16(8 * 2 * DFF), 2 * DFF)
        WB = Buf()
        src = wb_o if is_attn else wb_pw2
        fw.dma("sp", "wta", wproj, src[j].rearrange("(kc p) c -> p kc c", p=128), writes=[WB])
        for cb in range(11):
            fw.dma("sp", "wta", win[:, :, cb * 512:(cb + 1) * 512],
                   wb_in[l, :, cb * 512:(cb + 1) * 512].rearrange("(kc p) c -> p kc c", p=128), writes=[WB])
        inT = [(v3(sb.bf16(8 * TT), TT), Buf()) for _ in range(2)]
        inr = Ring(inT)
        xt = v3(sb.f32(8 * TT), TT)
        xtB = bufs(8)
        h2 = v3(sb.bf16(8 * TT), TT)
        h2B = bufs(8)
        aT = v3(sb.bf16(22 * TT), TT)
        aTB = bufs(22)
        gs = sb.f32(8)
        gb = sb.f32(8)
        modB = Buf()
        mod_gs(gs, ada(l, si, 4), col("n2g", l * 8, 8), modB)
        sh = ada(l, si, 3)
        g1 = ada(l, si, 2)
        if not is_attn:
            tt("dve", gb, g1, col("bpw2", j * 8, 8), ALU.mult, [adaB, coltabB], [modB])
        tmp = Ring([(sb.f32(TT), Buf()) for _ in range(2)])
        sqr = Ring([(sb.bf16(TT), Buf()) for _ in range(2)])
        sgr = Ring([(sb.f32(TT), Buf()) for _ in range(2)])
        r_ap, rB = sb.f32(TT), Buf()
        pbank = Ring([(ps[:, b, :], psB[b]) for b in range(0, 6)])
        for ti in range(S // TT):
            t0 = ti * TT
            it, itB = inr.next()
            fw.dma("sp", "tain%d" % (ti % 2), it, oT_s[:, :, t0:t0 + TT], writes=[itB])
            fw.dma("sp", "tax", xt, xT_s[:, :, t0:t0 + TT], writes=xtB)
            for oc in range(8):
                pb, pbB = pbank.next()
                for kc in range(8):
                    mm(pb, wproj[:, kc, oc * 128:(oc + 1) * 128], it[:, kc, :], kc == 0, kc == 7, [WB, itB], [pbB],
                       kc == 7)
                if is_attn:
                    stt("dve", xt[:, oc, :], pb, g1[:, oc:oc + 1], xt[:, oc, :], ALU.mult, ALU.add,
                        [pbB, adaB, xtB[oc]], [xtB[oc]])
                else:
                    tp, tpB = tmp.next()
                    act(tp, pb, AF.Identity, [pbB, adaB, modB], [tpB], bias=gb[:, oc:oc + 1], scale=g1[:, oc:oc + 1])
                    tt("dve", xt[:, oc, :], xt[:, oc, :], tp, ALU.add, [tpB, xtB[oc]], [xtB[oc]])
            fw.dma("pool", "stx", xT_s[:, :, t0:t0 + TT], xt, reads=xtB)
            norm_mod(xt, xtB, h2, h2B, gs, sh, modB, tmp, sqr, (ps[:, 6, :], psB[6]), r_ap, rB)
            for fc in range(22):
                pg, pgB = pbank.next()
                pu, puB = pbank.next()
                for kc in range(8):
                    mm(pg, win[:, kc, fc * 128:(fc + 1) * 128], h2[:, kc, :], kc == 0, kc == 7, [WB, h2B[kc]], [pgB],
                       kc == 7)
                for kc in range(8):
                    mm(pu, win[:, kc, DFF + fc * 128:DFF + (fc + 1) * 128], h2[:, kc, :], kc == 0, kc == 7,
                       [WB, h2B[kc]], [puB], kc == 7)
                sg, sgB = sgr.next()
                act(sg, pg, AF.Silu, [pgB], [sgB])
                tt("dve", aT[:, fc, :], pu, sg, ALU.mult, [puB, sgB], [aTB[fc]])
            fw.dma("pool", "sta", aT_s[:, 0:11, t0:t0 + TT], aT[:, 0:11, :], reads=aTB[0:11])
            fw.dma("pool", "sta", aT_s[:, 11:22, t0:t0 + TT], aT[:, 11:22, :], reads=aTB[11:22])
        sb.pop()

    def phase_tb(si, l):
        S = seqs[si][2]
        last = (l == NL - 1)
        nxt_conv = (not last) and ((l + 1) % 2 == 1)
        jn = (l + 1) // 2
        sb.push()
        wout = v3(sb.bf16(22 * D), D)
        WB = Buf()
        for half in range(2):
            fw.dma("sp", "wtb", wout[:, half * 11:(half + 1) * 11, :],
                   wb_out[l, half * 11 * 128:(half + 1) * 11 * 128, :].rearrange("(kc p) c -> p kc c", p=128),
                   writes=[WB])
        if nxt_conv:
            wpw1 = v3(sb.bf16(8 * 2 * D), 2 * D)
            for half in range(2):
                fw.dma("sp", "wtb", wpw1[:, :, half * D:(half + 1) * D],
                       wb_pw1[jn, :, half * D:(half + 1) * D].rearrange("(kc p) c -> p kc c", p=128), writes=[WB])
        aTt = [(v3(sb.bf16(22 * TT), TT), Buf()) for _ in range(2)]
        ar = Ring(aTt)
        xt = v3(sb.f32(8 * TT), TT)
        xtB = bufs(8)
        g2 = ada(l, si, 5)
        modB = Buf()
        tmp = Ring([(sb.f32(TT), Buf()) for _ in range(2)])
        pbank = Ring([(ps[:, b, :], psB[b]) for b in range(0, 4)])
        if last:
            yr = Ring([(sb.f32(D), Buf()) for _ in range(2)])
        else:
            hn = v3(sb.bf16(8 * TT), TT)
            hnB = bufs(8)
            gs = sb.f32(8)
            mod_gs(gs, ada(l + 1, si, 1), col("n1g", (l + 1) * 8, 8), modB)
            sh = ada(l + 1, si, 0)
            sqr = Ring([(sb.bf16(TT), Buf()) for _ in range(2)])
            r_ap, rB = sb.f32(TT), Buf()
            if nxt_conv:
                uT = v3(sb.bf16(8 * TT), TT)
                uTB = bufs(8)
                sgr = Ring([(sb.f32(TT), Buf()) for _ in range(2)])
        for ti in range(S // TT):
            t0 = ti * TT
            at, atB = ar.next()
            fw.dma("sp", "tbin%d" % (ti % 2), at[:, 0:11, :], aT_s[:, 0:11, t0:t0 + TT], writes=[atB])
            fw.dma("sp", "tbin%d" % (ti % 2), at[:, 11:22, :], aT_s[:, 11:22, t0:t0 + TT], writes=[atB])
            fw.dma("sp", "tbx", xt, xT_s[:, :, t0:t0 + TT], writes=xtB)
            for oc in range(8):
                pb, pbB = pbank.next()
                for fc in range(22):
                    mm(pb, wout[:, fc, oc * 128:(oc + 1) * 128], at[:, fc, :], fc == 0, fc == 21, [WB, atB], [pbB],
                       fc == 21)
                stt("dve", xt[:, oc, :], pb, g2[:, oc:oc + 1], xt[:, oc, :], ALU.mult, ALU.add,
                    [pbB, adaB, xtB[oc]], [xtB[oc]])
            if last:
                for s4 in range(4):
                    yt, ytB = yr.next()
                    for hf in range(2):
                        pb, pbB = (ps[:, 4 + hf, :], psB[4 + hf])
                        for c4 in range(4):
                            c = hf * 4 + c4
                            tr(pb[:, c4 * 128:(c4 + 1) * 128], xt[:, c, s4 * 128:(s4 + 1) * 128], ident,
                               [xtB[c], identB], [pbB], c4 == 3)
                        cp("act" if hf else "dve", yt[:, hf * 512:(hf + 1) * 512], pb, [pbB], [ytB])
                    fw.dma("pool", "sty%d" % (s4 % 2), yrows(si, t0 + s4 * 128, 128), yt, reads=[ytB])
            else:
                fw.dma("pool", "stx", xT_s[:, :, t0:t0 + TT], xt, reads=xtB)
                norm_mod(xt, xtB, hn, hnB, gs, sh, modB, tmp, sqr, (ps[:, 6, :], psB[6]), r_ap, rB)
                if not nxt_conv:
                    fw.dma("pool", "sth", hT_s[:, :, t0:t0 + TT], hn, reads=hnB)
                else:
                    for cc in range(8):
                        pa, paB = pbank.next()
                        pg, pgB = pbank.next()
                        for kc in range(8):
                            mm(pa, wpw1[:, kc, cc * 128:(cc + 1) * 128], hn[:, kc, :], kc == 0, kc == 7,
                               [WB, hnB[kc]], [paB], kc == 7)
                        for kc in range(8):
                            mm(pg, wpw1[:, kc, D + cc * 128:D + (cc + 1) * 128], hn[:, kc, :], kc == 0, kc == 7,
                               [WB, hnB[kc]], [pgB], kc == 7)
                        sg, sgB = sgr.next()
                        act(sg, pg, AF.Sigmoid, [pgB, coltabB], [sgB], bias=col("bpw1", jn * 16 + 8 + cc), scale=1.0)
                        stt("dve", uT[:, cc, :], pa, col("bpw1", jn * 16 + cc), sg, ALU.add, ALU.mult,
                            [paB, coltabB, sgB], [uTB[cc]])
                    fw.dma("pool", "stu", uT_s[:, :, t0:t0 + TT], uT, reads=uTB)
        sb.pop()

    def phase_conv(si, l):
        S = seqs[si][2]
        j = l // 2
        PAD = (CONVW - 1) // 2
        HW = TT + 2 * PAD
        sb.push()
        dg = sb.bf16(8 * CONVW * 128).rearrange("p (c t k) -> p c t k", c=8, t=CONVW)
        DGB = Buf()
        for c in range(8):
            fw.dma("sp", "dgl", dg[:, c, :, :],
                   dg_s[j, :, c * CONVW * 128:(c + 1) * CONVW * 128].rearrange("p (t k) -> p t k", k=128),
                   writes=[DGB])
        ut = [(v3(sb.bf16(8 * HW), HW), Buf()) for _ in range(2)]
        ur = Ring(ut)
        v = v3(sb.f32(8 * TT), TT)
        vB = bufs(8)
        vb = v3(sb.bf16(8 * TT), TT)
        vbB = bufs(8)
        sq = v3(sb.bf16(8 * TT), TT)
        sqB = bufs(8)
        z = v3(sb.bf16(8 * TT), TT)
        zB = bufs(8)
        mean, msq, var = sb.f32(TT), sb.f32(TT), sb.f32(TT)
        stB = Buf()
        t1r = Ring([(sb.f32(TT), Buf()) for _ in range(2)])
        t2r = Ring([(sb.f32(TT), Buf()) for _ in range(2)])
        pbank = Ring([(ps[:, b, :], psB[b]) for b in range(0, 4)])
        psum_, psumB = ps[:, 4, :], psB[4]
        psq_, psqB = ps[:, 5, :], psB[5]
        nt = S // TT
        for ti in range(nt):
            t0 = ti * TT
            u, uB = ur.next()
            lo = max(0, t0 - PAD)
            hi = min(S, t0 + TT + PAD)
            if ti == 0:
                memset("pool", u[:, :, 0:PAD], 0.0, [uB])
            if ti == nt - 1:
                memset("pool", u[:, :, HW - PAD:HW], 0.0, [uB])
            fw.dma("sp", "cvin%d" % (ti % 2), u[:, :, lo - (t0 - PAD):hi - (t0 - PAD)], uT_s[:, :, lo:hi],
                   writes=[uB])

            def stats_mm(c):
                mm(psum_, ones_b, vb[:, c, :], c == 0, c == 7, [vbB[c], constB], [psumB], True)
                mm(psq_, ones_b, sq[:, c, :], c == 0, c == 7, [sqB[c], constB], [psqB], True)

            for c in range(8):
                pb, pbB = pbank.next()
                for t in range(CONVW):
                    mm(pb, dg[:, c, t, :], u[:, c, t:t + TT], t == 0, t == CONVW - 1, [DGB, uB], [pbB],
                       t == CONVW - 1)
                if c >= 1:
                    stats_mm(c - 1)
                bcol = col("bdw", j * 8 + c)
                act(v[:, c, :], pb, AF.Identity, [pbB, coltabB], [vB[c]], bias=bcol, scale=1.0)
                act(sq[:, c, :], pb, AF.Square, [pbB, coltabB], [sqB[c]], bias=bcol, scale=1.0)
                cp("dve", vb[:, c, :], v[:, c, :], [vB[c]], [vbB[c]])
            stats_mm(7)
            ts("dve", mean, psum_, 1.0 / D, None, ALU.mult, None, [psumB], [stB])
            tt("dve", msq, mean, mean, ALU.mult, [stB], [stB])
            stt("dve", var, psq_, 1.0 / D, msq, ALU.mult, ALU.subtract, [psqB, stB], [stB])
            act(var, var, AF.Sqrt, [stB, constB], [stB], bias=epsc[:, 0:1], scale=1.0)
            recip(var, var, [stB], [stB])
            for c in range(8):
                t1, t1B = t1r.next()
                t2, t2B = t2r.next()
                tt("pool", t1, v[:, c, :], mean, ALU.subtract, [vB[c], stB], [t1B])
                tt("dve", t2, t1, var, ALU.mult, [t1B, stB], [t2B])
                act(z[:, c, :], t2, AF.Silu, [t2B, coltabB], [zB[c]], bias=col("lnb", j * 8 + c),
                    scale=col("lng", j * 8 + c))
            fw.dma("pool", "stz", oT_s[:, :, t0:t0 + TT], z, reads=zB)
        sb.pop()

    phases = []
    for si in range(NSEQ):
        phases.append((phase_in, (si,)))
        for l in range(NL):
            phases.append((phase_att if l % 2 == 0 else phase_conv, (si, l)))
            phases.append((phase_ta, (si, l)))
            phases.append((phase_tb, (si, l)))
    for k, (fn_, args_) in enumerate(phases):
        if debug is not None and k >= debug:
            break
        fn_(*args_)
    build_program.stats = dict(sb_peak=sb.peak, n_instr={k: len(e.q) for k, e in fw.E.items()}, nsem=len(fw.sems))

    fw.barrier()
    fw.run()
    return nc


_CACHE = {}


def kernel(x_prompt, x_sample, c_prompt, c_sample, rel_bias_table, norm1_g, norm2_g,
           ada_w, ada_b, attn_w_qkv, attn_q_gain, attn_k_gain, attn_w_o,
           conv_w_pw1, conv_b_pw1, conv_w_dw, conv_b_dw, conv_ln_g, conv_ln_b,
           conv_w_pw2, conv_b_pw2, ffn_w_in, ffn_w_out):
    NC = 8
    f = lambda a: np.ascontiguousarray(np.asarray(a, dtype=np.float32))
    x_prompt, x_sample, c_prompt, c_sample = f(x_prompt), f(x_sample), f(c_prompt), f(c_sample)
    BP, SP_, _ = x_prompt.shape
    BS, SS_, _ = x_sample.shape
    n_p, n_s = BP // NC, BS // NC
    key = (n_p, n_s, SP_, SS_)
    if key not in _CACHE:
        _CACHE[key] = build_program(n_p, n_s, SP_, SS_)
    nc = _CACHE[key]
    shared = {
        "rel_bias_table": f(rel_bias_table), "norm1_g": f(norm1_g), "norm2_g": f(norm2_g),
        "ada_w": f(ada_w), "ada_b": f(ada_b), "attn_w_qkv": f(attn_w_qkv),
        "attn_q_gain": f(attn_q_gain), "attn_k_gain": f(attn_k_gain), "attn_w_o": f(attn_w_o),
        "conv_w_pw1": f(conv_w_pw1), "conv_b_pw1": f(conv_b_pw1), "conv_w_dw": f(conv_w_dw),
        "conv_b_dw": f(conv_b_dw), "conv_ln_g": f(conv_ln_g), "conv_ln_b": f(conv_ln_b),
        "conv_w_pw2": f(conv_w_pw2), "conv_b_pw2": f(conv_b_pw2), "ffn_w_in": f(ffn_w_in),
        "ffn_w_out": f(ffn_w_out),
        "ident": np.eye(128, dtype=np.float32), "onehot": make_onehot(),
    }
    in_maps = []
    for c in range(NC):
        m = dict(shared)
        m["xp"] = x_prompt[c * n_p:(c + 1) * n_p]
        m["xs"] = x_sample[c * n_s:(c + 1) * n_s]
        m["cp"] = c_prompt[c * n_p:(c + 1) * n_p]
        m["cs"] = c_sample[c * n_s:(c + 1) * n_s]
        in_maps.append(m)
    res = run_bass_kernel_spmd(nc, in_maps, core_ids=list(range(NC)))
    yp = np.concatenate([np.asarray(r["yp"]) for r in res.results], axis=0).astype(np.float32)
    ys = np.concatenate([np.asarray(r["ys"]) for r in res.results], axis=0).astype(np.float32)
    return (yp, ys)
```

```python
import math
import numpy as np
import concourse.bass as bass
import concourse.mybir as mybir
from concourse.bass_utils import run_bass_kernel_spmd

F32 = mybir.dt.float32
BF16 = mybir.dt.bfloat16
AF = mybir.ActivationFunctionType
ALU = mybir.AluOpType

D = 1024
DFF = 2816
NL = 4
QKVW = 9216
EPS = 1e-6
GROUPS = ((128, 1), (512, 4), (2048, 16))
CONVW = 31
TT = 512


class Buf:
    __slots__ = ("w", "r")

    def __init__(self):
        self.w = None
        self.r = []


def bufs(n):
    return [Buf() for _ in range(n)]


class Eng:
    def __init__(self, name, sem, same_sync):
        self.name = name
        self.sem = sem
        self.cnt = 0
        self.seen = {}
        self.q = []
        self.same_sync = same_sync


class FW:
    def __init__(self, nc):
        self.nc = nc
        self.sems = []
        self.E = {}
        for name, same in (("pe", False), ("act", True), ("dve", True), ("pool", True), ("sp", False)):
            self.E[name] = Eng(name, self._sem("e_" + name), same)
        self.dsem = {}

    def _sem(self, name):
        cm = self.nc.semaphore(name)
        s = cm.__enter__()
        self.sems.append(cm)
        return s

    def _waits(self, eng, deps):
        need = {}
        for (s, v) in deps:
            if s is eng.sem and not eng.same_sync:
                continue
            k = id(s)
            if eng.seen.get(k, 0) >= v:
                continue
            if k not in need or need[k][1] < v:
                need[k] = (s, v)
        for k, (s, v) in need.items():
            eng.seen[k] = v
        return list(need.values())

    @staticmethod
    def _deps(reads, writes):
        deps = []
        for b in reads:
            if b.w is not None:
                deps.append(b.w)
        for b in writes:
            if b.w is not None:
                deps.append(b.w)
            deps.extend(b.r)
        return deps

    def op(self, en, fn, reads=(), writes=(), inc=True, relaxed=False):
        eng = self.E[en]
        deps = self._deps(reads, writes)
        if relaxed:
            deps = [d for d in deps if d[0] is not eng.sem]
        waits = self._waits(eng, deps)
        idx = eng.cnt + 1
        if inc:
            eng.cnt = idx
        tok = (eng.sem, idx)
        eng.q.append((waits, fn, eng.sem if inc else None, 1))
        for b in reads:
            b.r.append(tok)
        for b in writes:
            b.w = tok
            b.r = []
        return tok

    def dma(self, qn, slot, out, in_, reads=(), writes=()):
        eng = self.E[qn]
        if slot not in self.dsem:
            self.dsem[slot] = [self._sem("d_" + slot), 0]
        ds = self.dsem[slot]
        waits = self._waits(eng, self._deps(reads, writes))
        ds[1] += 16
        tok = (ds[0], ds[1])
        eng.q.append((waits, (lambda e, o=out, i=in_: e.dma_start(out=o, in_=i)), ds[0], 16))
        for b in reads:
            b.r.append(tok)
        for b in writes:
            b.w = tok
            b.r = []
        return tok

    def barrier(self):
        toks = []
        for e in self.E.values():
            if e.cnt > 0:
                toks.append((e.sem, e.cnt))
        for s, c in self.dsem.values():
            if c > 0:
                toks.append((s, c))
        for e in self.E.values():
            waits = []
            for (s, v) in toks:
                if s is e.sem:
                    continue
                k = id(s)
                if e.seen.get(k, 0) >= v:
                    continue
                e.seen[k] = v
                waits.append((s, v))
            if waits:
                e.q.append((waits, None, None, 0))

    def run(self):
        nc = self.nc
        with nc.Block() as block:
            def mk(en):
                def body(eng):
                    for (waits, fn, sem, inc) in self.E[en].q:
                        for (s, v) in waits:
                            eng.wait_ge(s, v)
                        if fn is not None:
                            ins = fn(eng)
                            if sem is not None:
                                ins.then_inc(sem, inc)
                return body
            block.tensor(mk("pe"))
            block.scalar(mk("act"))
            block.vector(mk("dve"))
            block.gpsimd(mk("pool"))
            block.sync(mk("sp"))


class Ring:
    def __init__(self, items):
        self.items = items
        self.i = 0

    def next(self):
        it = self.items[self.i % len(self.items)]
        self.i += 1
        return it


class SBAlloc:
    def __init__(self, nc, words):
        self.t = nc.alloc_sbuf_tensor("sb_all", [128, words], F32)
        self.words = words
        self.off = 0
        self.stack = []
        self.peak = 0
        self.on_pop = None

    def push(self):
        self.stack.append(self.off)

    def pop(self):
        self.off = self.stack.pop()
        if self.on_pop is not None:
            self.on_pop()

    def f32(self, n):
        ap = self.t[:, self.off:self.off + n]
        self.off += n
        self.peak = max(self.peak, self.off)
        assert self.off <= self.words, ("SBUF overflow", self.off, self.words)
        return ap

    def bf16(self, n):
        assert n % 2 == 0
        w = n // 2
        ap = self.t[:, self.off:self.off + w].bitcast(BF16)
        self.off += w
        self.peak = max(self.peak, self.off)
        assert self.off <= self.words, ("SBUF overflow", self.off, self.words)
        return ap


def v3(ap, b):
    return ap.rearrange("p (a b) -> p a b", b=b)


def _t5_bucket_np(rel):
    half = 16
    max_exact = 8
    ret = np.where(rel > 0, half, 0)
    n = np.abs(rel)
    nf = np.maximum(n, 1).astype(np.float32)
    large = max_exact + (np.log(nf / np.float32(max_exact)) / np.float32(math.log(1024 / max_exact))
                         * np.float32(half - max_exact)).astype(np.int32)
    large = np.minimum(large, half - 1)
    return ret + np.where(n < max_exact, n, large)


def make_onehot():
    oh = np.zeros((3, 32, 384), np.float32)
    j = np.arange(384)
    rel = 191 - j
    valid = np.abs(rel) <= 64
    for g, (win, dil) in enumerate(GROUPS):
        b = _t5_bucket_np(rel * dil)
        for jj in range(384):
            if valid[jj]:
                oh[g, b[jj], jj] = 1.0
    return oh


def build_program(n_p, n_s, s_p=4096, s_s=2048, debug=None):
    nc = bass.Bass("TRN2", target_bir_lowering=False)
    fw = FW(nc)
    NSEQ = n_p + n_s
    seqs = [("p", i, s_p) for i in range(n_p)] + [("s", i, s_s) for i in range(n_s)]
    SMAX = max(s for _, _, s in seqs)
    tokbase = []
    TOT = 0
    for _, _, s_ in seqs:
        tokbase.append(TOT)
        TOT += s_

    def ext_in(name, shape):
        return nc.dram_tensor(name, list(shape), F32, kind="ExternalInput").ap()

    xin_p = ext_in("xp", (n_p, s_p, D)) if n_p else None
    xin_s = ext_in("xs", (n_s, s_s, D)) if n_s else None
    c_p = ext_in("cp", (n_p, D)) if n_p else None
    c_s = ext_in("cs", (n_s, D)) if n_s else None
    rel_tab = ext_in("rel_bias_table", (32, 24))
    norm1_g = ext_in("norm1_g", (NL, D))
    norm2_g = ext_in("norm2_g", (NL, D))
    ada_w = ext_in("ada_w", (NL, D, 6 * D))
    ada_b = ext_in("ada_b", (NL, 6 * D))
    w_qkv = ext_in("attn_w_qkv", (2, D, QKVW))
    q_gain = ext_in("attn_q_gain", (2, 128))
    k_gain = ext_in("attn_k_gain", (2, 128))
    w_o = ext_in("attn_w_o", (2, D, D))
    w_pw1 = ext_in("conv_w_pw1", (2, D, 2 * D))
    b_pw1 = ext_in("conv_b_pw1", (2, 2 * D))
    w_dw = ext_in("conv_w_dw", (2, CONVW, D))
    b_dw = ext_in("conv_b_dw", (2, D))
    ln_g = ext_in("conv_ln_g", (2, D))
    ln_b = ext_in("conv_ln_b", (2, D))
    w_pw2 = ext_in("conv_w_pw2", (2, D, D))
    b_pw2 = ext_in("conv_b_pw2", (2, D))
    w_in = ext_in("ffn_w_in", (NL, D, 2 * DFF))
    w_out = ext_in("ffn_w_out", (NL, DFF, D))
    ident_in = ext_in("ident", (128, 128))
    oh_in = ext_in("onehot", (3, 32, 384))

    y_p = nc.dram_tensor("yp", [n_p, s_p, D], F32, kind="ExternalOutput").ap() if n_p else None
    y_s = nc.dram_tensor("ys", [n_s, s_s, D], F32, kind="ExternalOutput").ap() if n_s else None

    def scr(name, shape, dt):
        if debug is not None and name in ("xT_s", "hT_s", "oT_s", "uT_s", "aT_s"):
            return nc.dram_tensor(name, list(shape), dt, kind="ExternalOutput").ap()
        return nc.dram_tensor(name, list(shape), dt).ap()

    wb_qkv = scr("wb_qkv", (2, D, QKVW), BF16)
    wb_o = scr("wb_o", (2, D, D), BF16)
    wb_pw1 = scr("wb_pw1", (2, D, 2 * D), BF16)
    wb_pw2 = scr("wb_pw2", (2, D, D), BF16)
    wb_in = scr("wb_in", (NL, D, 2 * DFF), BF16)
    wb_out = scr("wb_out", (NL, DFF, D), BF16)
    xT_s = scr("xT_s", (128, 8, TOT), F32)
    hT_s = scr("hT_s", (128, 8, TOT), BF16)
    oT_s = scr("oT_s", (128, 8, TOT), BF16)
    uT_s = scr("uT_s", (128, 8, TOT), BF16)
    aT_s = scr("aT_s", (128, 22, TOT), BF16)
    dg_s = scr("dg_s", (2, 128, 8 * CONVW * 128), BF16)
    E_s = scr("E_s", (24, 128, 384), BF16)

    sb = SBAlloc(nc, 53000)
    sb.on_pop = fw.barrier
    ps = nc.alloc_psum_tensor("ps_all", [128, 8, 512], F32)
    psB = bufs(8)

    def dbg_dump(name, ap, n, dt, reads):
        if debug is None:
            return
        t = nc.dram_tensor(name, [128, n], dt, kind="ExternalOutput").ap()
        fw.dma("pool", "dbg", t[:, :], ap, reads=reads)

    def mm(out, lhsT, rhs, start, stop, reads, writes, inc):
        fw.op("pe", lambda e: e.matmul(out, lhsT, rhs, start=start, stop=stop), reads, writes, inc)

    def tr(out, in_, idn, reads, writes, inc):
        fw.op("pe", lambda e: e.transpose(out=out, in_=in_, identity=idn), reads, writes, inc)

    def act(out, in_, func, reads, writes, bias=None, scale=None):
        kw = {}
        if bias is not None:
            kw["bias"] = bias
        if scale is not None:
            kw["scale"] = scale
        fw.op("act", lambda e: e.activation(out=out, in_=in_, func=func, **kw), reads, writes)

    def tt(en, out, in0, in1, op, reads, writes, relaxed=False):
        fw.op(en, lambda e: e.tensor_tensor(out=out, in0=in0, in1=in1, op=op), reads, writes, relaxed=relaxed)

    def ts(en, out, in0, s1, s2, op0, op1, reads, writes):
        if s2 is None:
            fw.op(en, lambda e: e.tensor_scalar(out=out, in0=in0, scalar1=s1, scalar2=None, op0=op0), reads, writes)
        else:
            fw.op(en, lambda e: e.tensor_scalar(out=out, in0=in0, scalar1=s1, scalar2=s2, op0=op0, op1=op1), reads, writes)

    def stt(en, out, in0, scalar, in1, op0, op1, reads, writes, relaxed=False):
        fw.op(en, lambda e: e.scalar_tensor_tensor(out=out, in0=in0, scalar=scalar, in1=in1, op0=op0, op1=op1), reads,
              writes, relaxed=relaxed)

    def cp(en, out, in_, reads, writes, relaxed=False):
        if en == "act":
            act(out, in_, AF.Copy, reads, writes)
        else:
            fw.op(en, lambda e: e.tensor_copy(out=out, in_=in_), reads, writes, relaxed=relaxed)

    def recip(out, in_, reads, writes):
        fw.op("dve", lambda e: e.reciprocal(out=out, in_=in_), reads, writes)

    def memset(en, ap, val, writes):
        fw.op(en, lambda e: e.memset(ap, val), (), writes)

    ident = sb.f32(128)
    identB = Buf()
    ones_b = sb.bf16(128)
    ones_f = sb.f32(128)
    constB = Buf()
    epsc = sb.f32(2)
    fw.dma("sp", "misc0", ident, ident_in[:, :], writes=[identB])
    memset("dve", ones_b, 1.0, [constB])
    memset("dve", ones_f, 1.0, [constB])
    memset("dve", epsc, EPS, [constB])

    entries = []
    if n_p:
        entries.append(("c_p", c_p.rearrange("s (c p) -> (s c) p", p=128)))
    if n_s:
        entries.append(("c_s", c_s.rearrange("s (c p) -> (s c) p", p=128)))
    entries += [
        ("ada_b", ada_b.rearrange("l (c p) -> (l c) p", p=128)),
        ("n1g", norm1_g.rearrange("l (c p) -> (l c) p", p=128)),
        ("n2g", norm2_g.rearrange("l (c p) -> (l c) p", p=128)),
        ("bpw1", b_pw1.rearrange("l (c p) -> (l c) p", p=128)),
        ("wdw", w_dw.rearrange("l t (c p) -> (l t c) p", p=128)),
        ("bdw", b_dw.rearrange("l (c p) -> (l c) p", p=128)),
        ("lng", ln_g.rearrange("l (c p) -> (l c) p", p=128)),
        ("lnb", ln_b.rearrange("l (c p) -> (l c) p", p=128)),
        ("bpw2", b_pw2.rearrange("l (c p) -> (l c) p", p=128)),
        ("qg", q_gain),
        ("kg", k_gain),
    ]
    ncols = sum(a.shape[0] for _, a in entries)
    coltab = sb.f32(ncols)
    coltabB = Buf()
    colbase = {}
    sb.push()
    stage = [(sb.f32(128), Buf()) for _ in range(2)]
    stg = Ring(stage)
    pbank = Ring([(ps[:, b, :], psB[b]) for b in range(2)])
    base = 0
    for name, ap in entries:
        colbase[name] = base
        rows = ap.shape[0]
        r0 = 0
        while r0 < rows:
            n = min(128, rows - r0)
            st, stB = stg.next()
            pb, pbB = pbank.next()
            fw.dma("sp", "stage%d" % (stg.i % 2), st[0:n, :], ap[r0:r0 + n, :], writes=[stB])
            tr(pb[:, 0:n], st[0:n, :], ident[0:n, 0:n], [stB, identB], [pbB], True)
            cp("act", coltab[:, base + r0:base + r0 + n], pb[:, 0:n], [pbB], [coltabB])
            r0 += n
        base += rows
    sb.pop()

    def col(name, idx, n=1):
        b0 = colbase[name] + idx
        return coltab[:, b0:b0 + n]

    cbase = colbase["c_p"] if n_p else colbase["c_s"]

    adaT = sb.f32(NL * NSEQ * 48)
    adaB = Buf()
    sb.push()
    cact = sb.f32(NSEQ * 8)
    cactB = Buf()
    act(cact, coltab[:, cbase:cbase + NSEQ * 8], AF.Silu, [coltabB], [cactB])
    awt = [(v3(sb.f32(8 * 512), 512), Buf()) for _ in range(2)]
    awr = Ring(awt)
    for l in range(NL):
        pa, paB = ps[:, 2 + (l % 2), :], psB[2 + (l % 2)]
        for cb in range(12):
            wt, wtB = awr.next()
            fw.dma("sp", "adaw%d" % (awr.i % 2), wt, ada_w[l, :, cb * 512:(cb + 1) * 512].rearrange("(kc p) c -> p kc c", p=128),
                   writes=[wtB])
            for jj in range(4):
                j = cb * 4 + jj
                for kc in range(8):
                    mm(pa[:, j * NSEQ:(j + 1) * NSEQ], wt[:, kc, jj * 128:(jj + 1) * 128],
                       cact[:, kc:kc + 8 * (NSEQ - 1) + 1:8], kc == 0, kc == 7,
                       [wtB, cactB], [paB], (jj == 3 and kc == 7))
        for s in range(NSEQ):
            o0 = (l * NSEQ + s) * 48
            tt("dve", adaT[:, o0:o0 + 48], pa[:, s:s + NSEQ * 47 + 1:NSEQ], col("ada_b", l * 48, 48), ALU.add,
               [paB, coltabB], [adaB])
    sb.pop()

    def ada(l, s, k):
        o0 = (l * NSEQ + s) * 48 + k * 8
        return adaT[:, o0:o0 + 8]

    sb.push()
    tab = sb.f32(24)
    tabB = Buf()
    fw.dma("sp", "misc1", tab[0:32, :], rel_tab[:, :], writes=[tabB])
    act(tab[0:32, :], tab[0:32, :], AF.Exp, [tabB], [tabB])
    oh = v3(sb.f32(3 * 384), 384)
    ohB = Buf()
    fw.dma("sp", "misc2", oh[0:32, :, :], oh_in.rearrange("g b j -> b g j"), writes=[ohB])
    ebs = [(sb.f32(128), Buf()) for _ in range(2)]
    ebr = Ring(ebs)
    dts = [(sb.bf16(384), Buf()) for _ in range(2)]
    dtr = Ring(dts)
    pbank = Ring([(ps[:, b, :], psB[b]) for b in range(4, 6)])
    for gh in range(24):
        g = gh // 8
        eb, ebB = ebr.next()
        ts("dve", eb[0:32, :], ones_f[0:32, :], tab[0:32, gh:gh + 1], None, ALU.mult, None, [constB, tabB], [ebB])
        pb, pbB = pbank.next()
        mm(pb[:, 0:384], eb[0:32, :], oh[0:32, g, :], True, True, [ebB, ohB], [pbB], True)
        dt_, dtB = dtr.next()
        cp("act", dt_, pb[:, 0:384], [pbB], [dtB])
        fw.dma("pool", "est%d" % (gh % 2), E_s[gh], dt_, reads=[dtB])
    sb.pop()

    sb.push()
    FW_ = 2048
    cin = [(sb.f32(FW_), Buf()) for _ in range(3)]
    cout = [(sb.bf16(FW_), Buf()) for _ in range(3)]
    cinr, coutr = Ring(cin), Ring(cout)
    k = 0
    for src, dst in ((w_qkv, wb_qkv), (w_o, wb_o), (w_pw1, wb_pw1), (w_pw2, wb_pw2), (w_in, wb_in), (w_out, wb_out)):
        sf = src.rearrange("l r c -> (l r c)").rearrange("(n p f) -> n p f", p=128, f=FW_)
        df = dst.rearrange("l r c -> (l r c)").rearrange("(n p f) -> n p f", p=128, f=FW_)
        for n in range(sf.shape[0]):
            ci, ciB = cinr.next()
            co, coB = coutr.next()
            fw.dma("sp", "cvi%d" % (k % 3), ci, sf[n], writes=[ciB])
            cp(("act", "dve", "pool")[k % 3], co, ci, [ciB], [coB])
            fw.dma("pool", "cvo%d" % (k % 3), df[n], co, reads=[coB])
            k += 1
    sb.pop()

    sb.push()
    identb = sb.bf16(128)
    identbB = Buf()
    cp("dve", identb, ident, [identB], [identbB])
    dgt = [(sb.bf16(CONVW * 128), Buf()) for _ in range(2)]
    dgr = Ring(dgt)
    for j in range(2):
        for c in range(8):
            dgb, dgB = dgr.next()
            for t in range(CONVW):
                ts("dve", dgb[:, t * 128:(t + 1) * 128], identb, col("wdw", (j * CONVW + t) * 8 + c), None,
                   ALU.mult, None, [identbB, coltabB], [dgB])
            fw.dma("pool", "dgst%d" % (c % 2), dg_s[j, :, c * CONVW * 128:(c + 1) * CONVW * 128], dgb, reads=[dgB])
    sb.pop()

    def norm_mod(xt, xtB, ht, htB, gs, sh, modB, tmp, sq_ring, ss_bank, r_ap, rB):
        ssp, sspB = ss_bank
        for c in range(8):
            sq, sqB = sq_ring.next()
            act(sq, xt[:, c, :], AF.Square, [xtB[c]], [sqB])
            mm(ssp, ones_b, sq, c == 0, c == 7, [sqB, constB], [sspB], True)
        act(r_ap, ssp, AF.Sqrt, [sspB, constB], [rB], bias=epsc[:, 0:1], scale=1.0 / D)
        recip(r_ap, r_ap, [rB], [rB])
        for c in range(8):
            tp, tpB = tmp.next()
            stt("dve", tp, xt[:, c, :], gs[:, c:c + 1], r_ap, ALU.mult, ALU.mult, [xtB[c], modB, rB], [tpB])
            act(ht[:, c, :], tp, AF.Identity, [tpB, modB], [htB[c]], bias=sh[:, c:c + 1], scale=1.0)

    def mod_gs(dst, sc, gcol, modB):
        stt("dve", dst, sc, 1.0, gcol, ALU.add, ALU.mult, [adaB, coltabB], [modB])

    def xrows(si, t0, n):
        kind, i, S = seqs[si]
        src = xin_p if kind == "p" else xin_s
        return src[i, t0:t0 + n, :]

    def yrows(si, t0, n):
        kind, i, S = seqs[si]
        dst = y_p if kind == "p" else y_s
        return dst[i, t0:t0 + n, :]

    def phase_in():
        sb.push()
        xin = [(v3(sb.f32(4 * D), D), Buf()) for _ in range(2)]
        xinr = Ring(xin)
        xt = v3(sb.f32(8 * TT), TT)
        xtB = bufs(8)
        ht = v3(sb.bf16(8 * TT), TT)
        htB = bufs(8)
        gs = sb.f32(8)
        modB = Buf()
        tmp = Ring([(sb.f32(TT), Buf()) for _ in range(2)])
        sqr = Ring([(sb.bf16(TT), Buf()) for _ in range(2)])
        r_ap, rB = sb.f32(TT), Buf()
        pbank = Ring([(ps[:, b, :], psB[b]) for b in range(0, 4)])
        for si in range(NSEQ):
            S = seqs[si][2]
            tb = tokbase[si]
            mod_gs(gs, ada(0, si, 1), col("n1g", 0, 8), modB)
            sh = ada(0, si, 0)
            for ti in range(S // TT):
                t0 = ti * TT
                xi, xiB = xinr.next()
                for s4 in range(4):
                    fw.dma("sp", "in%d" % (xinr.i % 2), xi[:, s4, :], xrows(si, t0 + s4 * 128, 128), writes=[xiB])
                for c in range(8):
                    pb, pbB = pbank.next()
                    for s4 in range(4):
                        tr(pb[:, s4 * 128:(s4 + 1) * 128], xi[:, s4, c * 128:(c + 1) * 128], ident, [xiB, identB],
                           [pbB], s4 == 3)
                    cp("act" if c % 2 else "dve", xt[:, c, :], pb, [pbB], [xtB[c]])
                fw.dma("pool", "stx", xT_s[:, :, tb + t0:tb + t0 + TT], xt, reads=xtB)
                norm_mod(xt, xtB, ht, htB, gs, sh, modB, tmp, sqr, (ps[:, 6, :], psB[6]), r_ap, rB)
                fw.dma("pool", "sth", hT_s[:, :, tb + t0:tb + t0 + TT], ht, reads=htB)
        sb.pop()

    def phase_att(l):
        j = l // 2
        sb.push()
        E = v3(sb.bf16(24 * 256), 256)
        EB = Buf()
        for gh in range(24):
            fw.dma("sp", "eld", E[:, gh, :], bass.AP(E_s.tensor, gh * 128 * 384 + 127, [[383, 128], [1, 256]]),
                   writes=[EB])
        gq = sb.f32(2)
        gqB = Buf()
        ts("dve", gq[:, 0:1], col("qg", j), 128.0 ** -0.5, None, ALU.mult, None, [coltabB], [gqB])
        ts("dve", gq[:, 1:2], col("kg", j), 1.0, None, ALU.mult, None, [coltabB], [gqB])
        for si in range(NSEQ):
            att_seq(si, l, E, EB, gq, gqB)
        sb.pop()

    def att_seq(si, l, E, EB, gq, gqB):
        S = seqs[si][2]
        tb = tokbase[si]
        j = l // 2
        ntt = S // TT
        sb.push()
        hT = v3(sb.bf16(8 * S), S)
        hTall = Buf()
        for ti in range(ntt):
            fw.dma("sp", "hld", hT[:, :, ti * TT:(ti + 1) * TT], hT_s[:, :, tb + ti * TT:tb + (ti + 1) * TT],
                   writes=[hTall])
        wsl = [(sb.bf16(3 * 8 * 128).rearrange("p (t k c) -> p t k c", t=3, k=8), Buf()) for _ in range(2)]
        wr = Ring(wsl)
        qT, kT = sb.bf16(S), sb.bf16(S)
        qkAll = [Buf(), Buf()]
        V = v3(sb.bf16(S), 128)
        nkt = S // 128
        VB = bufs(nkt // 4)
        OS3 = v3(sb.f32(2 * S), S)
        OSB = Buf()
        oout = sb.bf16(S)
        ooutB = Buf()
        sqr = Ring([(sb.bf16(TT), Buf()) for _ in range(2)])
        rawr = Ring([(sb.f32(TT), Buf()) for _ in range(3)])
        rr = Ring([(sb.f32(TT), Buf()) for _ in range(3)])
        exr = Ring([(sb.bf16(256), Buf()) for _ in range(3)])
        ptr_ = Ring([(sb.bf16(256), Buf()) for _ in range(3)])
        pq = Ring([(ps[:, b, :], psB[b]) for b in (0, 1)])
        pss = (ps[:, 2, :], psB[2])
        pv = (ps[:, 3, :], psB[3])
        pst = Ring([(ps[:, b, 0:256], psB[b]) for b in (4, 5, 3)])
        po = Ring([(ps[:, b, 0:256], psB[b]) for b in (6, 7)])

        for h in range(8):
            for g, (win, dil) in enumerate(GROUPS):
                gh = g * 8 + h
                L = S // dil
                nt = L // 128
                w, wB = wr.next()
                for t in range(3):
                    c0 = t * 3072 + g * 1024 + h * 128
                    fw.dma("sp", "aw%d" % (wr.i % 2), w[:, t, :, :],
                           wb_qkv[j, :, c0:c0 + 128].rearrange("(kc p) c -> p kc c", p=128), writes=[wB])
                K2 = 2 * ntt
                st_ = {}
                for k in range(K2 + 2):
                    if k < K2:
                        ti, t = k // 2, k % 2
                        pb, pbB = pq.next()
                        for kc in range(8):
                            mm(pb, w[:, t, kc, :], hT[:, kc, ti * TT:(ti + 1) * TT], kc == 0, kc == 7,
                               [wB, hTall], [pbB], kc == 7)
                        sq, sqB = sqr.next()
                        raw, rawB = rawr.next()
                        act(sq, pb, AF.Square, [pbB], [sqB])
                        act(raw, pb, AF.Identity, [pbB, gqB], [rawB], scale=gq[:, t:t + 1])
                        st_[k] = [sq, sqB, raw, rawB, None, None]
                    if 1 <= k <= K2:
                        e = st_[k - 1]
                        r_, rB_ = rr.next()
                        e[4], e[5] = r_, rB_
                        mm(pss[0], ones_b, e[0], True, True, [e[1], constB], [pss[1]], True)
                        act(r_, pss[0], AF.Sqrt, [pss[1], constB], [rB_], bias=epsc[:, 0:1], scale=1.0 / 128.0)
                        recip(r_, r_, [rB_], [rB_])
                    if 2 <= k <= K2 + 1:
                        kk = k - 2
                        e = st_.pop(kk)
                        ti, t = kk // 2, kk % 2
                        dst = (qT, kT)[t]
                        tt("pool", dst[:, ti * TT:(ti + 1) * TT], e[2], e[4], ALU.mult,
                           [e[3], e[5]], [qkAll[t]], relaxed=True)
                tiles = [(r, i) for r in range(dil) for i in range(nt)]
                for v0 in range(0, nkt, 4):
                    for u in range(4):
                        r, i = tiles[v0 + u]
                        a0 = r + dil * 128 * i
                        for kc in range(8):
                            mm(pv[0][:, u * 128:(u + 1) * 128], hT[:, kc, a0:a0 + dil * 127 + 1:dil], w[:, 2, kc, :],
                               kc == 0, kc == 7, [wB, hTall], [pv[1]], (u == 3 and kc == 7))
                    cp("act", V[:, v0:v0 + 4, :], v3(pv[0], 128), [pv[1]], [VB[v0 // 4]])

                def qk(n):
                    r, i = tiles[n]
                    qlo = max(0, 128 * i - 64)
                    qhi = min(L, 128 * i + 192)
                    nq = qhi - qlo
                    sp_, spB = pst.next()
                    k0 = r + dil * 128 * i
                    q0 = r + dil * qlo
                    mm(sp_[:, 0:nq], kT[:, k0:k0 + dil * 127 + 1:dil], qT[:, q0:q0 + dil * (nq - 1) + 1:dil],
                       True, True, qkAll, [spB], True)
                    return (sp_, spB, qlo, nq)

                pend = {}
                for n in range(min(2, len(tiles))):
                    pend[n] = qk(n)
                prev = None
                for n in range(len(tiles)):
                    if n + 2 < len(tiles):
                        pend[n + 2] = qk(n + 2)
                    r, i = tiles[n]
                    sp_, spB, qlo, nq = pend.pop(n)
                    off = qlo - (128 * i - 64)
                    ex, exB = exr.next()
                    pt, ptB = ptr_.next()
                    act(ex[:, 0:nq], sp_[:, 0:nq], AF.Exp, [spB], [exB])
                    tt("pool", pt[:, 0:nq], ex[:, 0:nq], E[:, gh, off:off + nq], ALU.mult, [exB, EB], [ptB])
                    if i == 0:
                        prev = None
                    cur = (n, pt, ptB, qlo)
                    done = [i] + ([nt] if i == nt - 1 else [])
                    for reg in done:
                        qa, qb = max(0, 128 * reg - 64), min(L, 128 * reg + 64)
                        wd = qb - qa
                        contrib = []
                        if reg >= 1:
                            pn, ppt, pptB, pqlo = prev if reg == i else cur
                            contrib.append((pn, ppt, pptB, qa - pqlo))
                        if reg <= nt - 1:
                            contrib.append((n, pt, ptB, qa - qlo))
                        pr, prB = po.next()
                        for ci, (vn, cpt, cptB, c0) in enumerate(contrib):
                            mm(pr[:, 0:wd], V[:, vn, :], cpt[:, c0:c0 + wd], ci == 0, ci == len(contrib) - 1,
                               [VB[vn // 4], cptB], [prB], False)
                        for ci, (vn, cpt, cptB, c0) in enumerate(contrib):
                            mm(pr[:, 128:128 + wd], ones_b, cpt[:, c0:c0 + wd], ci == 0, ci == len(contrib) - 1,
                               [cptB, constB], [prB], ci == len(contrib) - 1)
                        p0 = r + dil * qa
                        osl = OS3[:, :, p0:p0 + dil * (wd - 1) + 1:dil]
                        pr3 = v3(pr, 128)[:, :, 0:wd]
                        if g == 0:
                            cp("dve", osl, pr3, [prB], [OSB], relaxed=True)
                        else:
                            tt("dve", osl, pr3, osl, ALU.add, [prB, OSB], [OSB], relaxed=True)
                    prev = cur
            for c0 in range(0, S, 2048):
                recip(OS3[:, 1, c0:c0 + 2048], OS3[:, 1, c0:c0 + 2048], [OSB], [OSB])
                tt("pool", oout[:, c0:c0 + 2048], OS3[:, 0, c0:c0 + 2048], OS3[:, 1, c0:c0 + 2048], ALU.mult,
                   [OSB], [ooutB])
            fw.dma("pool", "sto", oT_s[:, h, tb:tb + S], oout, reads=[ooutB])
        if l == 0 and si == 0:
            dbg_dump("d_qT", qT, S, BF16, [qkAll[0]])
            dbg_dump("d_kT", kT, S, BF16, [qkAll[1]])
            dbg_dump("d_V", V.rearrange("p a b -> p (a b)"), S, BF16, VB)
            dbg_dump("d_E", E.rearrange("p a b -> p (a b)"), 24 * 256, BF16, [EB])
            dbg_dump("d_O", OS3[:, 0, :], S, F32, [OSB])
            dbg_dump("d_s", OS3[:, 1, :], S, F32, [OSB])
            dbg_dump("d_oo", oout, S, BF16, [ooutB])
        sb.pop()

    def phase_ta(l):
        j = l // 2
        is_attn = (l % 2 == 0)
        sb.push()
        wproj = v3(sb.bf16(8 * D), D)
        win = v3(sb.bf16(8 * 2 * DFF), 2 * DFF)
        WB = Buf()
        src = wb_o if is_attn else wb_pw2
        fw.dma("sp", "wta", wproj, src[j].rearrange("(kc p) c -> p kc c", p=128), writes=[WB])
        for cb in range(11):
            fw.dma("sp", "wta", win[:, :, cb * 512:(cb + 1) * 512],
                   wb_in[l, :, cb * 512:(cb + 1) * 512].rearrange("(kc p) c -> p kc c", p=128), writes=[WB])
        inT = [(v3(sb.bf16(8 * TT), TT), Buf()) for _ in range(2)]
        inr = Ring(inT)
        xt = v3(sb.f32(8 * TT), TT)
        xtB = bufs(8)
        h2 = v3(sb.bf16(8 * TT), TT)
        h2B = bufs(8)
        aT = v3(sb.bf16(22 * TT), TT)
        aTB = bufs(22)
        gs = sb.f32(8)
        gb = sb.f32(8)
        modB = Buf()
        tmp = Ring([(sb.f32(TT), Buf()) for _ in range(2)])
        sqr = Ring([(sb.bf16(TT), Buf()) for _ in range(2)])
        sgr = Ring([(sb.f32(TT), Buf()) for _ in range(2)])
        r_ap, rB = sb.f32(TT), Buf()
        pbank = Ring([(ps[:, b, :], psB[b]) for b in range(0, 6)])
        for si in range(NSEQ):
            S = seqs[si][2]
            tb = tokbase[si]
            mod_gs(gs, ada(l, si, 4), col("n2g", l * 8, 8), modB)
            sh = ada(l, si, 3)
            g1 = ada(l, si, 2)
            if not is_attn:
                tt("dve", gb, g1, col("bpw2", j * 8, 8), ALU.mult, [adaB, coltabB], [modB])
            for ti in range(S // TT):
                t0 = tb + ti * TT
                it, itB = inr.next()
                fw.dma("sp", "tain%d" % (inr.i % 2), it, oT_s[:, :, t0:t0 + TT], writes=[itB])
                fw.dma("sp", "tax", xt, xT_s[:, :, t0:t0 + TT], writes=xtB)
                for oc in range(8):
                    pb, pbB = pbank.next()
                    for kc in range(8):
                        mm(pb, wproj[:, kc, oc * 128:(oc + 1) * 128], it[:, kc, :], kc == 0, kc == 7, [WB, itB],
                           [pbB], kc == 7)
                    if is_attn:
                        stt("dve", xt[:, oc, :], pb, g1[:, oc:oc + 1], xt[:, oc, :], ALU.mult, ALU.add,
                            [pbB, adaB, xtB[oc]], [xtB[oc]])
                    else:
                        tp, tpB = tmp.next()
                        act(tp, pb, AF.Identity, [pbB, adaB, modB], [tpB], bias=gb[:, oc:oc + 1],
                            scale=g1[:, oc:oc + 1])
                        tt("dve", xt[:, oc, :], xt[:, oc, :], tp, ALU.add, [tpB, xtB[oc]], [xtB[oc]])
                fw.dma("pool", "stx", xT_s[:, :, t0:t0 + TT], xt, reads=xtB)
                norm_mod(xt, xtB, h2, h2B, gs, sh, modB, tmp, sqr, (ps[:, 6, :], psB[6]), r_ap, rB)
                for fc in range(22):
                    pg, pgB = pbank.next()
                    pu, puB = pbank.next()
                    for kc in range(8):
                        mm(pg, win[:, kc, fc * 128:(fc + 1) * 128], h2[:, kc, :], kc == 0, kc == 7, [WB, h2B[kc]],
                           [pgB], kc == 7)
                    for kc in range(8):
                        mm(pu, win[:, kc, DFF + fc * 128:DFF + (fc + 1) * 128], h2[:, kc, :], kc == 0, kc == 7,
                           [WB, h2B[kc]], [puB], kc == 7)
                    sg, sgB = sgr.next()
                    act(sg, pg, AF.Silu, [pgB], [sgB])
                    tt("dve", aT[:, fc, :], pu, sg, ALU.mult, [puB, sgB], [aTB[fc]])
                fw.dma("pool", "sta0", aT_s[:, 0:11, t0:t0 + TT], aT[:, 0:11, :], reads=aTB[0:11])
                fw.dma("pool", "sta1", aT_s[:, 11:22, t0:t0 + TT], aT[:, 11:22, :], reads=aTB[11:22])
        sb.pop()

    def phase_tb(l):
        last = (l == NL - 1)
        nxt_conv = (not last) and ((l + 1) % 2 == 1)
        jn = (l + 1) // 2
        sb.push()
        wout = v3(sb.bf16(22 * D), D)
        WB = Buf()
        for half in range(2):
            fw.dma("sp", "wtb", wout[:, half * 11:(half + 1) * 11, :],
                   wb_out[l, half * 11 * 128:(half + 1) * 11 * 128, :].rearrange("(kc p) c -> p kc c", p=128),
                   writes=[WB])
        if nxt_conv:
            wpw1 = v3(sb.bf16(8 * 2 * D), 2 * D)
            for half in range(2):
                fw.dma("sp", "wtb", wpw1[:, :, half * D:(half + 1) * D],
                       wb_pw1[jn, :, half * D:(half + 1) * D].rearrange("(kc p) c -> p kc c", p=128), writes=[WB])
        aTt = [(v3(sb.bf16(22 * TT), TT), Buf()) for _ in range(2)]
        ar = Ring(aTt)
        xt = v3(sb.f32(8 * TT), TT)
        xtB = bufs(8)
        modB = Buf()
        tmp = Ring([(sb.f32(TT), Buf()) for _ in range(2)])
        pbank = Ring([(ps[:, b, :], psB[b]) for b in range(0, 4)])
        if last:
            yr = Ring([(sb.f32(D), Buf()) for _ in range(2)])
        else:
            hn = v3(sb.bf16(8 * TT), TT)
            hnB = bufs(8)
            gs = sb.f32(8)
            sqr = Ring([(sb.bf16(TT), Buf()) for _ in range(2)])
            r_ap, rB = sb.f32(TT), Buf()
            if nxt_conv:
                uT = v3(sb.bf16(8 * TT), TT)
                uTB = bufs(8)
                sgr = Ring([(sb.f32(TT), Buf()) for _ in range(2)])
        for si in range(NSEQ):
            S = seqs[si][2]
            tb = tokbase[si]
            g2 = ada(l, si, 5)
            if not last:
                mod_gs(gs, ada(l + 1, si, 1), col("n1g", (l + 1) * 8, 8), modB)
                sh = ada(l + 1, si, 0)
            for ti in range(S // TT):
                t0 = tb + ti * TT
                at, atB = ar.next()
                fw.dma("sp", "tbin%d" % (ar.i % 2), at[:, 0:11, :], aT_s[:, 0:11, t0:t0 + TT], writes=[atB])
                fw.dma("sp", "tbin%d" % (ar.i % 2), at[:, 11:22, :], aT_s[:, 11:22, t0:t0 + TT], writes=[atB])
                fw.dma("sp", "tbx", xt, xT_s[:, :, t0:t0 + TT], writes=xtB)
                for oc in range(8):
                    pb, pbB = pbank.next()
                    for fc in range(22):
                        mm(pb, wout[:, fc, oc * 128:(oc + 1) * 128], at[:, fc, :], fc == 0, fc == 21, [WB, atB],
                           [pbB], fc == 21)
                    stt("dve", xt[:, oc, :], pb, g2[:, oc:oc + 1], xt[:, oc, :], ALU.mult, ALU.add,
                        [pbB, adaB, xtB[oc]], [xtB[oc]])
                if last:
                    for s4 in range(4):
                        yt, ytB = yr.next()
                        for hf in range(2):
                            pb, pbB = (ps[:, 4 + hf, :], psB[4 + hf])
                            for c4 in range(4):
                                c = hf * 4 + c4
                                tr(pb[:, c4 * 128:(c4 + 1) * 128], xt[:, c, s4 * 128:(s4 + 1) * 128], ident,
                                   [xtB[c], identB], [pbB], c4 == 3)
                            cp("act" if hf else "dve", yt[:, hf * 512:(hf + 1) * 512], pb, [pbB], [ytB])
                        fw.dma("pool", "sty%d" % (yr.i % 2), yrows(si, ti * TT + s4 * 128, 128), yt, reads=[ytB])
                else:
                    fw.dma("pool", "stx", xT_s[:, :, t0:t0 + TT], xt, reads=xtB)
                    norm_mod(xt, xtB, hn, hnB, gs, sh, modB, tmp, sqr, (ps[:, 6, :], psB[6]), r_ap, rB)
                    if not nxt_conv:
                        fw.dma("pool", "sth", hT_s[:, :, t0:t0 + TT], hn, reads=hnB)
                    else:
                        for cc in range(8):
                            pa, paB = pbank.next()
                            pg, pgB = pbank.next()
                            for kc in range(8):
                                mm(pa, wpw1[:, kc, cc * 128:(cc + 1) * 128], hn[:, kc, :], kc == 0, kc == 7,
                                   [WB, hnB[kc]], [paB], kc == 7)
                            for kc in range(8):
                                mm(pg, wpw1[:, kc, D + cc * 128:D + (cc + 1) * 128], hn[:, kc, :], kc == 0, kc == 7,
                                   [WB, hnB[kc]], [pgB], kc == 7)
                            sg, sgB = sgr.next()
                            act(sg, pg, AF.Sigmoid, [pgB, coltabB], [sgB], bias=col("bpw1", jn * 16 + 8 + cc),
                                scale=1.0)
                            stt("dve", uT[:, cc, :], pa, col("bpw1", jn * 16 + cc), sg, ALU.add, ALU.mult,
                                [paB, coltabB, sgB], [uTB[cc]])
                        fw.dma("pool", "stu", uT_s[:, :, t0:t0 + TT], uT, reads=uTB)
        sb.pop()

    def phase_conv(l):
        j = l // 2
        PAD = (CONVW - 1) // 2
        HW = TT + 2 * PAD
        sb.push()
        dg = sb.bf16(8 * CONVW * 128).rearrange("p (c t k) -> p c t k", c=8, t=CONVW)
        DGB = Buf()
        for c in range(8):
            fw.dma("sp", "dgl", dg[:, c, :, :],
                   dg_s[j, :, c * CONVW * 128:(c + 1) * CONVW * 128].rearrange("p (t k) -> p t k", k=128),
                   writes=[DGB])
        ut = [(v3(sb.bf16(8 * HW), HW), Buf()) for _ in range(2)]
        ur = Ring(ut)
        v = v3(sb.f32(8 * TT), TT)
        vB = bufs(8)
        vb = v3(sb.bf16(8 * TT), TT)
        vbB = bufs(8)
        sq = v3(sb.bf16(8 * TT), TT)
        sqB = bufs(8)
        z = v3(sb.bf16(8 * TT), TT)
        zB = bufs(8)
        mean, msq, var = sb.f32(TT), sb.f32(TT), sb.f32(TT)
        stB = Buf()
        t1r = Ring([(sb.f32(TT), Buf()) for _ in range(2)])
        t2r = Ring([(sb.f32(TT), Buf()) for _ in range(2)])
        pbank = Ring([(ps[:, b, :], psB[b]) for b in range(0, 4)])
        psum_, psumB = ps[:, 4, :], psB[4]
        psq_, psqB = ps[:, 5, :], psB[5]
        for si in range(NSEQ):
            S = seqs[si][2]
            tb = tokbase[si]
            nt = S // TT
            for ti in range(nt):
                t0 = ti * TT
                u, uB = ur.next()
                lo = max(0, t0 - PAD)
                hi = min(S, t0 + TT + PAD)
                if ti == 0:
                    memset("pool", u[:, :, 0:PAD], 0.0, [uB])
                if ti == nt - 1:
                    memset("pool", u[:, :, HW - PAD:HW], 0.0, [uB])
                fw.dma("sp", "cvin%d" % (ur.i % 2), u[:, :, lo - (t0 - PAD):hi - (t0 - PAD)],
                       uT_s[:, :, tb + lo:tb + hi], writes=[uB])

                def stats_mm(c):
                    mm(psum_, ones_b, vb[:, c, :], c == 0, c == 7, [vbB[c], constB], [psumB], True)
                    mm(psq_, ones_b, sq[:, c, :], c == 0, c == 7, [sqB[c], constB], [psqB], True)

                for c in range(8):
                    pb, pbB = pbank.next()
                    for t in range(CONVW):
                        mm(pb, dg[:, c, t, :], u[:, c, t:t + TT], t == 0, t == CONVW - 1, [DGB, uB], [pbB],
                           t == CONVW - 1)
                    if c >= 1:
                        stats_mm(c - 1)
                    bcol = col("bdw", j * 8 + c)
                    act(v[:, c, :], pb, AF.Identity, [pbB, coltabB], [vB[c]], bias=bcol, scale=1.0)
                    act(sq[:, c, :], pb, AF.Square, [pbB, coltabB], [sqB[c]], bias=bcol, scale=1.0)
                    cp("dve", vb[:, c, :], v[:, c, :], [vB[c]], [vbB[c]])
                stats_mm(7)
                ts("dve", mean, psum_, 1.0 / D, None, ALU.mult, None, [psumB], [stB])
                tt("dve", msq, mean, mean, ALU.mult, [stB], [stB])
                stt("dve", var, psq_, 1.0 / D, msq, ALU.mult, ALU.subtract, [psqB, stB], [stB])
                act(var, var, AF.Sqrt, [stB, constB], [stB], bias=epsc[:, 0:1], scale=1.0)
                recip(var, var, [stB], [stB])
                for c in range(8):
                    t1, t1B = t1r.next()
                    t2, t2B = t2r.next()
                    tt("pool", t1, v[:, c, :], mean, ALU.subtract, [vB[c], stB], [t1B])
                    tt("dve", t2, t1, var, ALU.mult, [t1B, stB], [t2B])
                    act(z[:, c, :], t2, AF.Silu, [t2B, coltabB], [zB[c]], bias=col("lnb", j * 8 + c),
                        scale=col("lng", j * 8 + c))
                fw.dma("pool", "stz", oT_s[:, :, tb + t0:tb + t0 + TT], z, reads=zB)
        sb.pop()

    phases = [(phase_in, ())]
    for l in range(NL):
        phases.append((phase_att if l % 2 == 0 else phase_conv, (l,)))
        phases.append((phase_ta, (l,)))
        phases.append((phase_tb, (l,)))
    for k, (fn_, args_) in enumerate(phases):
        if debug is not None and k >= debug:
            break
        fn_(*args_)
    build_program.stats = dict(sb_peak=sb.peak, n_instr={k: len(e.q) for k, e in fw.E.items()}, nsem=len(fw.sems))

    fw.barrier()
    fw.run()
    return nc


_CACHE = {}


def kernel(x_prompt, x_sample, c_prompt, c_sample, rel_bias_table, norm1_g, norm2_g,
           ada_w, ada_b, attn_w_qkv, attn_q_gain, attn_k_gain, attn_w_o,
           conv_w_pw1, conv_b_pw1, conv_w_dw, conv_b_dw, conv_ln_g, conv_ln_b,
           conv_w_pw2, conv_b_pw2, ffn_w_in, ffn_w_out):
    NC = 8
    f = lambda a: np.ascontiguousarray(np.asarray(a, dtype=np.float32))
    x_prompt, x_sample, c_prompt, c_sample = f(x_prompt), f(x_sample), f(c_prompt), f(c_sample)
    BP, SP_, _ = x_prompt.shape
    BS, SS_, _ = x_sample.shape
    n_p, n_s = BP // NC, BS // NC
    key = (n_p, n_s, SP_, SS_)
    if key not in _CACHE:
        _CACHE[key] = build_program(n_p, n_s, SP_, SS_)
    nc = _CACHE[key]
    shared = {
        "rel_bias_table": f(rel_bias_table), "norm1_g": f(norm1_g), "norm2_g": f(norm2_g),
        "ada_w": f(ada_w), "ada_b": f(ada_b), "attn_w_qkv": f(attn_w_qkv),
        "attn_q_gain": f(attn_q_gain), "attn_k_gain": f(attn_k_gain), "attn_w_o": f(attn_w_o),
        "conv_w_pw1": f(conv_w_pw1), "conv_b_pw1": f(conv_b_pw1), "conv_w_dw": f(conv_w_dw),
        "conv_b_dw": f(conv_b_dw), "conv_ln_g": f(conv_ln_g), "conv_ln_b": f(conv_ln_b),
        "conv_w_pw2": f(conv_w_pw2), "conv_b_pw2": f(conv_b_pw2), "ffn_w_in": f(ffn_w_in),
        "ffn_w_out": f(ffn_w_out),
        "ident": np.eye(128, dtype=np.float32), "onehot": make_onehot(),
    }
    in_maps = []
    for c in range(NC):
        m = dict(shared)
        m["xp"] = x_prompt[c * n_p:(c + 1) * n_p]
        m["xs"] = x_sample[c * n_s:(c + 1) * n_s]
        m["cp"] = c_prompt[c * n_p:(c + 1) * n_p]
        m["cs"] = c_sample[c * n_s:(c + 1) * n_s]
        in_maps.append(m)
    res = run_bass_kernel_spmd(nc, in_maps, core_ids=list(range(NC)))
    yp = np.concatenate([np.asarray(r["yp"]) for r in res.results], axis=0).astype(np.float32)
    ys = np.concatenate([np.asarray(r["ys"]) for r in res.results], axis=0).astype(np.float32)
    return (yp, ys)
```

```python
import math
import numpy as np
import concourse.bass as bass
import concourse.mybir as mybir
from concourse.bass_utils import run_bass_kernel_spmd

F32 = mybir.dt.float32
BF16 = mybir.dt.bfloat16
AF = mybir.ActivationFunctionType
ALU = mybir.AluOpType

D = 1024
DFF = 2816
NL = 4
QKVW = 9216
EPS = 1e-6
GROUPS = ((128, 1), (512, 4), (2048, 16))
CONVW = 31
TT = 512


class Buf:
    __slots__ = ("w", "r")

    def __init__(self):
        self.w = None
        self.r = []


def bufs(n):
    return [Buf() for _ in range(n)]


class Eng:
    def __init__(self, name, sem, same_sync):
        self.name = name
        self.sem = sem
        self.cnt = 0
        self.seen = {}
        self.q = []
        self.same_sync = same_sync


class FW:
    def __init__(self, nc):
        self.nc = nc
        self.sems = []
        self.E = {}
        for name, same in (("pe", False), ("act", True), ("dve", True), ("pool", True), ("sp", False)):
            self.E[name] = Eng(name, self._sem("e_" + name), same)
        self.dsem = {}

    def _sem(self, name):
        cm = self.nc.semaphore(name)
        s = cm.__enter__()
        self.sems.append(cm)
        return s

    def _waits(self, eng, deps):
        need = {}
        for (s, v) in deps:
            if s is eng.sem and not eng.same_sync:
                continue
            k = id(s)
            if eng.seen.get(k, 0) >= v:
                continue
            if k not in need or need[k][1] < v:
                need[k] = (s, v)
        for k, (s, v) in need.items():
            eng.seen[k] = v
        return list(need.values())

    @staticmethod
    def _deps(reads, writes):
        deps = []
        for b in reads:
            if b.w is not None:
                deps.append(b.w)
        for b in writes:
            if b.w is not None:
                deps.append(b.w)
            deps.extend(b.r)
        return deps

    def op(self, en, fn, reads=(), writes=(), inc=True, relaxed=False):
        eng = self.E[en]
        deps = self._deps(reads, writes)
        if relaxed:
            deps = [d for d in deps if d[0] is not eng.sem]
        waits = self._waits(eng, deps)
        idx = eng.cnt + 1
        if inc:
            eng.cnt = idx
        tok = (eng.sem, idx)
        eng.q.append((waits, fn, eng.sem if inc else None, 1))
        for b in reads:
            b.r.append(tok)
        for b in writes:
            b.w = tok
            b.r = []
        return tok

    def dma(self, qn, slot, out, in_, reads=(), writes=()):
        eng = self.E[qn]
        if slot not in self.dsem:
            self.dsem[slot] = [self._sem("d_" + slot), 0]
        ds = self.dsem[slot]
        waits = self._waits(eng, self._deps(reads, writes))
        ds[1] += 16
        tok = (ds[0], ds[1])
        eng.q.append((waits, (lambda e, o=out, i=in_: e.dma_start(out=o, in_=i)), ds[0], 16))
        for b in reads:
            b.r.append(tok)
        for b in writes:
            b.w = tok
            b.r = []
        return tok

    def barrier(self):
        toks = []
        for e in self.E.values():
            if e.cnt > 0:
                toks.append((e.sem, e.cnt))
        for s, c in self.dsem.values():
            if c > 0:
                toks.append((s, c))
        for e in self.E.values():
            waits = []
            for (s, v) in toks:
                if s is e.sem:
                    continue
                k = id(s)
                if e.seen.get(k, 0) >= v:
                    continue
                e.seen[k] = v
                waits.append((s, v))
            if waits:
                e.q.append((waits, None, None, 0))

    def run(self):
        nc = self.nc
        with nc.Block() as block:
            def mk(en):
                def body(eng):
                    for (waits, fn, sem, inc) in self.E[en].q:
                        for (s, v) in waits:
                            eng.wait_ge(s, v)
                        if fn is not None:
                            ins = fn(eng)
                            if sem is not None:
                                ins.then_inc(sem, inc)
                return body
            block.tensor(mk("pe"))
            block.scalar(mk("act"))
            block.vector(mk("dve"))
            block.gpsimd(mk("pool"))
            block.sync(mk("sp"))


class Ring:
    def __init__(self, items):
        self.items = items
        self.i = 0

    def next(self):
        it = self.items[self.i % len(self.items)]
        self.i += 1
        return it


class SBAlloc:
    def __init__(self, nc, words):
        self.t = nc.alloc_sbuf_tensor("sb_all", [128, words], F32)
        self.words = words
        self.off = 0
        self.stack = []
        self.peak = 0
        self.on_pop = None

    def push(self):
        self.stack.append(self.off)

    def pop(self):
        self.off = self.stack.pop()
        if self.on_pop is not None:
            self.on_pop()

    def f32(self, n):
        ap = self.t[:, self.off:self.off + n]
        self.off += n
        self.peak = max(self.peak, self.off)
        assert self.off <= self.words, ("SBUF overflow", self.off, self.words)
        return ap

    def bf16(self, n):
        assert n % 2 == 0
        w = n // 2
        ap = self.t[:, self.off:self.off + w].bitcast(BF16)
        self.off += w
        self.peak = max(self.peak, self.off)
        assert self.off <= self.words, ("SBUF overflow", self.off, self.words)
        return ap


def v3(ap, b):
    return ap.rearrange("p (a b) -> p a b", b=b)


def _t5_bucket_np(rel):
    half = 16
    max_exact = 8
    ret = np.where(rel > 0, half, 0)
    n = np.abs(rel)
    nf = np.maximum(n, 1).astype(np.float32)
    large = max_exact + (np.log(nf / np.float32(max_exact)) / np.float32(math.log(1024 / max_exact))
                         * np.float32(half - max_exact)).astype(np.int32)
    large = np.minimum(large, half - 1)
    return ret + np.where(n < max_exact, n, large)


def make_onehot():
    oh = np.zeros((3, 32, 384), np.float32)
    j = np.arange(384)
    rel = 191 - j
    valid = np.abs(rel) <= 64
    for g, (win, dil) in enumerate(GROUPS):
        b = _t5_bucket_np(rel * dil)
        for jj in range(384):
            if valid[jj]:
                oh[g, b[jj], jj] = 1.0
    return oh


def build_program(n_p, n_s, s_p=4096, s_s=2048, debug=None):
    nc = bass.Bass("TRN2", target_bir_lowering=False)
    fw = FW(nc)
    NSEQ = n_p + n_s
    seqs = [("p", i, s_p) for i in range(n_p)] + [("s", i, s_s) for i in range(n_s)]
    SMAX = max(s for _, _, s in seqs)
    tokbase = []
    TOT = 0
    for _, _, s_ in seqs:
        tokbase.append(TOT)
        TOT += s_

    def ext_in(name, shape):
        return nc.dram_tensor(name, list(shape), F32, kind="ExternalInput").ap()

    xin_p = ext_in("xp", (n_p, s_p, D)) if n_p else None
    xin_s = ext_in("xs", (n_s, s_s, D)) if n_s else None
    c_p = ext_in("cp", (n_p, D)) if n_p else None
    c_s = ext_in("cs", (n_s, D)) if n_s else None
    rel_tab = ext_in("rel_bias_table", (32, 24))
    norm1_g = ext_in("norm1_g", (NL, D))
    norm2_g = ext_in("norm2_g", (NL, D))
    ada_w = ext_in("ada_w", (NL, D, 6 * D))
    ada_b = ext_in("ada_b", (NL, 6 * D))
    w_qkv = ext_in("attn_w_qkv", (2, D, QKVW))
    q_gain = ext_in("attn_q_gain", (2, 128))
    k_gain = ext_in("attn_k_gain", (2, 128))
    w_o = ext_in("attn_w_o", (2, D, D))
    w_pw1 = ext_in("conv_w_pw1", (2, D, 2 * D))
    b_pw1 = ext_in("conv_b_pw1", (2, 2 * D))
    w_dw = ext_in("conv_w_dw", (2, CONVW, D))
    b_dw = ext_in("conv_b_dw", (2, D))
    ln_g = ext_in("conv_ln_g", (2, D))
    ln_b = ext_in("conv_ln_b", (2, D))
    w_pw2 = ext_in("conv_w_pw2", (2, D, D))
    b_pw2 = ext_in("conv_b_pw2", (2, D))
    w_in = ext_in("ffn_w_in", (NL, D, 2 * DFF))
    w_out = ext_in("ffn_w_out", (NL, DFF, D))
    ident_in = ext_in("ident", (128, 128))
    oh_in = ext_in("onehot", (3, 32, 384))

    y_p = nc.dram_tensor("yp", [n_p, s_p, D], F32, kind="ExternalOutput").ap() if n_p else None
    y_s = nc.dram_tensor("ys", [n_s, s_s, D], F32, kind="ExternalOutput").ap() if n_s else None

    def scr(name, shape, dt):
        if debug is not None and name in ("xT_s", "hT_s", "oT_s", "uT_s", "aT_s"):
            return nc.dram_tensor(name, list(shape), dt, kind="ExternalOutput").ap()
        return nc.dram_tensor(name, list(shape), dt).ap()

    wb_qkv = scr("wb_qkv", (2, D, QKVW), BF16)
    wb_o = scr("wb_o", (2, D, D), BF16)
    wb_pw1 = scr("wb_pw1", (2, D, 2 * D), BF16)
    wb_pw2 = scr("wb_pw2", (2, D, D), BF16)
    wb_in = scr("wb_in", (NL, D, 2 * DFF), BF16)
    wb_out = scr("wb_out", (NL, DFF, D), BF16)
    xT_s = scr("xT_s", (128, 8, TOT), F32)
    hT_s = scr("hT_s", (128, 8, TOT), BF16)
    oT_s = scr("oT_s", (128, 8, TOT), BF16)
    uT_s = scr("uT_s", (128, 8, TOT), BF16)
    aT_s = scr("aT_s", (128, 22, TOT), BF16)
    dg_s = scr("dg_s", (2, 128, 8 * CONVW * 128), BF16)
    E_s = scr("E_s", (24, 128, 384), BF16)

    sb = SBAlloc(nc, 53000)
    sb.on_pop = fw.barrier
    ps = nc.alloc_psum_tensor("ps_all", [128, 8, 512], F32)
    psB = bufs(8)

    def dbg_dump(name, ap, n, dt, reads):
        if debug is None:
            return
        t = nc.dram_tensor(name, [128, n], dt, kind="ExternalOutput").ap()
        fw.dma("pool", "dbg", t[:, :], ap, reads=reads)

    def mm(out, lhsT, rhs, start, stop, reads, writes, inc):
        fw.op("pe", lambda e: e.matmul(out, lhsT, rhs, start=start, stop=stop), reads, writes, inc)

    def tr(out, in_, idn, reads, writes, inc):
        fw.op("pe", lambda e: e.transpose(out=out, in_=in_, identity=idn), reads, writes, inc)

    def act(out, in_, func, reads, writes, bias=None, scale=None):
        kw = {}
        if bias is not None:
            kw["bias"] = bias
        if scale is not None:
            kw["scale"] = scale
        fw.op("act", lambda e: e.activation(out=out, in_=in_, func=func, **kw), reads, writes)

    def tt(en, out, in0, in1, op, reads, writes, relaxed=False):
        fw.op(en, lambda e: e.tensor_tensor(out=out, in0=in0, in1=in1, op=op), reads, writes, relaxed=relaxed)

    def ts(en, out, in0, s1, s2, op0, op1, reads, writes):
        if s2 is None:
            fw.op(en, lambda e: e.tensor_scalar(out=out, in0=in0, scalar1=s1, scalar2=None, op0=op0), reads, writes)
        else:
            fw.op(en, lambda e: e.tensor_scalar(out=out, in0=in0, scalar1=s1, scalar2=s2, op0=op0, op1=op1), reads, writes)

    def stt(en, out, in0, scalar, in1, op0, op1, reads, writes, relaxed=False):
        fw.op(en, lambda e: e.scalar_tensor_tensor(out=out, in0=in0, scalar=scalar, in1=in1, op0=op0, op1=op1), reads,
              writes, relaxed=relaxed)

    def cp(en, out, in_, reads, writes, relaxed=False):
        if en == "act":
            act(out, in_, AF.Copy, reads, writes)
        else:
            fw.op(en, lambda e: e.tensor_copy(out=out, in_=in_), reads, writes, relaxed=relaxed)

    def recip(out, in_, reads, writes):
        fw.op("dve", lambda e: e.reciprocal(out=out, in_=in_), reads, writes)

    def memset(en, ap, val, writes):
        fw.op(en, lambda e: e.memset(ap, val), (), writes)

    ident = sb.f32(128)
    identB = Buf()
    ones_b = sb.bf16(128)
    ones_f = sb.f32(128)
    constB = Buf()
    epsc = sb.f32(2)
    fw.dma("sp", "misc0", ident, ident_in[:, :], writes=[identB])
    identb_p = sb.bf16(128)
    identbpB = Buf()
    fw.op("dve", lambda e: e.tensor_copy(out=identb_p, in_=ident), [identB], [identbpB])
    memset("dve", ones_b, 1.0, [constB])
    memset("dve", ones_f, 1.0, [constB])
    memset("dve", epsc, EPS, [constB])

    entries = []
    if n_p:
        entries.append(("c_p", c_p.rearrange("s (c p) -> (s c) p", p=128)))
    if n_s:
        entries.append(("c_s", c_s.rearrange("s (c p) -> (s c) p", p=128)))
    entries += [
        ("ada_b", ada_b.rearrange("l (c p) -> (l c) p", p=128)),
        ("n1g", norm1_g.rearrange("l (c p) -> (l c) p", p=128)),
        ("n2g", norm2_g.rearrange("l (c p) -> (l c) p", p=128)),
        ("bpw1", b_pw1.rearrange("l (c p) -> (l c) p", p=128)),
        ("wdw", w_dw.rearrange("l t (c p) -> (l t c) p", p=128)),
        ("bdw", b_dw.rearrange("l (c p) -> (l c) p", p=128)),
        ("lng", ln_g.rearrange("l (c p) -> (l c) p", p=128)),
        ("lnb", ln_b.rearrange("l (c p) -> (l c) p", p=128)),
        ("bpw2", b_pw2.rearrange("l (c p) -> (l c) p", p=128)),
        ("qg", q_gain),
        ("kg", k_gain),
    ]
    ncols = sum(a.shape[0] for _, a in entries)
    coltab = sb.f32(ncols)
    coltabB = Buf()
    colbase = {}
    sb.push()
    stage = [(sb.f32(128), Buf()) for _ in range(2)]
    stg = Ring(stage)
    pbank = Ring([(ps[:, b, :], psB[b]) for b in range(2)])
    base = 0
    for name, ap in entries:
        colbase[name] = base
        rows = ap.shape[0]
        r0 = 0
        while r0 < rows:
            n = min(128, rows - r0)
            st, stB = stg.next()
            pb, pbB = pbank.next()
            fw.dma("sp", "stage%d" % (stg.i % 2), st[0:n, :], ap[r0:r0 + n, :], writes=[stB])
            tr(pb[:, 0:n], st[0:n, :], ident[0:n, 0:n], [stB, identB], [pbB], True)
            cp("act", coltab[:, base + r0:base + r0 + n], pb[:, 0:n], [pbB], [coltabB])
            r0 += n
        base += rows
    sb.pop()

    def col(name, idx, n=1):
        b0 = colbase[name] + idx
        return coltab[:, b0:b0 + n]

    cbase = colbase["c_p"] if n_p else colbase["c_s"]

    adaT = sb.f32(NL * NSEQ * 48)
    adaB = Buf()
    sb.push()
    cact = sb.f32(NSEQ * 8)
    cactB = Buf()
    act(cact, coltab[:, cbase:cbase + NSEQ * 8], AF.Silu, [coltabB], [cactB])
    awt = [(v3(sb.f32(8 * 512), 512), Buf()) for _ in range(2)]
    awr = Ring(awt)
    for l in range(NL):
        pa, paB = ps[:, 2 + (l % 2), :], psB[2 + (l % 2)]
        for cb in range(12):
            wt, wtB = awr.next()
            fw.dma("sp", "adaw%d" % (awr.i % 2), wt, ada_w[l, :, cb * 512:(cb + 1) * 512].rearrange("(kc p) c -> p kc c", p=128),
                   writes=[wtB])
            for jj in range(4):
                j = cb * 4 + jj
                for kc in range(8):
                    mm(pa[:, j * NSEQ:(j + 1) * NSEQ], wt[:, kc, jj * 128:(jj + 1) * 128],
                       cact[:, kc:kc + 8 * (NSEQ - 1) + 1:8], kc == 0, kc == 7,
                       [wtB, cactB], [paB], (jj == 3 and kc == 7))
        for s in range(NSEQ):
            o0 = (l * NSEQ + s) * 48
            tt("dve", adaT[:, o0:o0 + 48], pa[:, s:s + NSEQ * 47 + 1:NSEQ], col("ada_b", l * 48, 48), ALU.add,
               [paB, coltabB], [adaB])
    sb.pop()

    def ada(l, s, k):
        o0 = (l * NSEQ + s) * 48 + k * 8
        return adaT[:, o0:o0 + 8]

    sb.push()
    tab = sb.f32(24)
    tabB = Buf()
    fw.dma("sp", "misc1", tab[0:32, :], rel_tab[:, :], writes=[tabB])
    act(tab[0:32, :], tab[0:32, :], AF.Exp, [tabB], [tabB])
    oh = v3(sb.f32(3 * 384), 384)
    ohB = Buf()
    fw.dma("sp", "misc2", oh[0:32, :, :], oh_in.rearrange("g b j -> b g j"), writes=[ohB])
    ebs = [(sb.f32(128), Buf()) for _ in range(2)]
    ebr = Ring(ebs)
    dts = [(sb.bf16(384), Buf()) for _ in range(2)]
    dtr = Ring(dts)
    pbank = Ring([(ps[:, b, :], psB[b]) for b in range(4, 6)])
    for gh in range(24):
        g = gh // 8
        eb, ebB = ebr.next()
        ts("dve", eb[0:32, :], ones_f[0:32, :], tab[0:32, gh:gh + 1], None, ALU.mult, None, [constB, tabB], [ebB])
        pb, pbB = pbank.next()
        mm(pb[:, 0:384], eb[0:32, :], oh[0:32, g, :], True, True, [ebB, ohB], [pbB], True)
        dt_, dtB = dtr.next()
        cp("act", dt_, pb[:, 0:384], [pbB], [dtB])
        fw.dma("pool", "est%d" % (gh % 2), E_s[gh], dt_, reads=[dtB])
    sb.pop()

    sb.push()
    FW_ = 2048
    cin = [(sb.f32(FW_), Buf()) for _ in range(3)]
    cout = [(sb.bf16(FW_), Buf()) for _ in range(3)]
    cinr, coutr = Ring(cin), Ring(cout)
    k = 0
    for src, dst in ((w_qkv, wb_qkv), (w_o, wb_o), (w_pw1, wb_pw1), (w_pw2, wb_pw2), (w_in, wb_in), (w_out, wb_out)):
        sf = src.rearrange("l r c -> (l r c)").rearrange("(n p f) -> n p f", p=128, f=FW_)
        df = dst.rearrange("l r c -> (l r c)").rearrange("(n p f) -> n p f", p=128, f=FW_)
        for n in range(sf.shape[0]):
            ci, ciB = cinr.next()
            co, coB = coutr.next()
            fw.dma("sp", "cvi%d" % (k % 3), ci, sf[n], writes=[ciB])
            cp(("act", "dve", "pool")[k % 3], co, ci, [ciB], [coB])
            fw.dma("pool", "cvo%d" % (k % 3), df[n], co, reads=[coB])
            k += 1
    sb.pop()

    sb.push()
    identb = sb.bf16(128)
    identbB = Buf()
    cp("dve", identb, ident, [identB], [identbB])
    dgt = [(sb.bf16(CONVW * 128), Buf()) for _ in range(2)]
    dgr = Ring(dgt)
    for j in range(2):
        for c in range(8):
            dgb, dgB = dgr.next()
            for t in range(CONVW):
                ts("dve", dgb[:, t * 128:(t + 1) * 128], identb, col("wdw", (j * CONVW + t) * 8 + c), None,
                   ALU.mult, None, [identbB, coltabB], [dgB])
            fw.dma("pool", "dgst%d" % (c % 2), dg_s[j, :, c * CONVW * 128:(c + 1) * CONVW * 128], dgb, reads=[dgB])
    sb.pop()

    def norm_mod(xt, xtB, ht, htB, gs, sh, modB, tmp, sq_ring, ss_bank, r_ap, rB):
        ssp, sspB = ss_bank
        for c in range(8):
            sq, sqB = sq_ring.next()
            act(sq, xt[:, c, :], AF.Square, [xtB[c]], [sqB])
            mm(ssp, ones_b, sq, c == 0, c == 7, [sqB, constB], [sspB], True)
        act(r_ap, ssp, AF.Sqrt, [sspB, constB], [rB], bias=epsc[:, 0:1], scale=1.0 / D)
        recip(r_ap, r_ap, [rB], [rB])
        for c in range(8):
            tp, tpB = tmp.next()
            stt("dve", tp, xt[:, c, :], gs[:, c:c + 1], r_ap, ALU.mult, ALU.mult, [xtB[c], modB, rB], [tpB])
            act(ht[:, c, :], tp, AF.Identity, [tpB, modB], [htB[c]], bias=sh[:, c:c + 1], scale=1.0)

    def mod_gs(dst, sc, gcol, modB):
        stt("dve", dst, sc, 1.0, gcol, ALU.add, ALU.mult, [adaB, coltabB], [modB])

    def xrows(si, t0, n):
        kind, i, S = seqs[si]
        src = xin_p if kind == "p" else xin_s
        return src[i, t0:t0 + n, :]

    def yrows(si, t0, n):
        kind, i, S = seqs[si]
        dst = y_p if kind == "p" else y_s
        return dst[i, t0:t0 + n, :]

    def phase_in():
        sb.push()
        xin = [(v3(sb.f32(4 * D), D), Buf()) for _ in range(2)]
        xinr = Ring(xin)
        xt = v3(sb.f32(8 * TT), TT)
        xtB = bufs(8)
        ht = v3(sb.bf16(8 * TT), TT)
        htB = bufs(8)
        gs = sb.f32(8)
        modB = Buf()
        tmp = Ring([(sb.f32(TT), Buf()) for _ in range(2)])
        sqr = Ring([(sb.bf16(TT), Buf()) for _ in range(2)])
        r_ap, rB = sb.f32(TT), Buf()
        pbank = Ring([(ps[:, b, :], psB[b]) for b in range(0, 4)])
        for si in range(NSEQ):
            S = seqs[si][2]
            tb = tokbase[si]
            mod_gs(gs, ada(0, si, 1), col("n1g", 0, 8), modB)
            sh = ada(0, si, 0)
            for ti in range(S // TT):
                t0 = ti * TT
                xi, xiB = xinr.next()
                for s4 in range(4):
                    fw.dma("sp", "in%d" % (xinr.i % 2), xi[:, s4, :], xrows(si, t0 + s4 * 128, 128), writes=[xiB])
                for c in range(8):
                    pb, pbB = pbank.next()
                    for s4 in range(4):
                        tr(pb[:, s4 * 128:(s4 + 1) * 128], xi[:, s4, c * 128:(c + 1) * 128], ident, [xiB, identB],
                           [pbB], s4 == 3)
                    cp("act" if c % 2 else "dve", xt[:, c, :], pb, [pbB], [xtB[c]])
                fw.dma("pool", "stx", xT_s[:, :, tb + t0:tb + t0 + TT], xt, reads=xtB)
                norm_mod(xt, xtB, ht, htB, gs, sh, modB, tmp, sqr, (ps[:, 6, :], psB[6]), r_ap, rB)
                fw.dma("pool", "sth", hT_s[:, :, tb + t0:tb + t0 + TT], ht, reads=htB)
        sb.pop()

    def phase_att(l):
        j = l // 2
        sb.push()
        E = v3(sb.bf16(24 * 256), 256)
        EB = Buf()
        for gh in range(24):
            fw.dma("sp", "eld", E[:, gh, :], bass.AP(E_s.tensor, gh * 128 * 384 + 127, [[383, 128], [1, 256]]),
                   writes=[EB])
        gq = sb.f32(2)
        gqB = Buf()
        ts("dve", gq[:, 0:1], col("qg", j), 128.0 ** -0.5, None, ALU.mult, None, [coltabB], [gqB])
        ts("dve", gq[:, 1:2], col("kg", j), 1.0, None, ALU.mult, None, [coltabB], [gqB])
        for si in range(NSEQ):
            att_seq(si, l, E, EB, gq, gqB)
        sb.pop()

    def att_seq(si, l, E, EB, gq, gqB):
        S = seqs[si][2]
        tb = tokbase[si]
        j = l // 2
        ntt = S // TT
        sb.push()
        hT = v3(sb.bf16(8 * S), S)
        hTall = Buf()
        for ti in range(ntt):
            fw.dma("sp", "hld", hT[:, :, ti * TT:(ti + 1) * TT], hT_s[:, :, tb + ti * TT:tb + (ti + 1) * TT],
                   writes=[hTall])
        wsl = [(sb.bf16(3 * 8 * 128).rearrange("p (t k c) -> p t k c", t=3, k=8), Buf()) for _ in range(2)]
        wr = Ring(wsl)
        qT, kT = sb.bf16(S), sb.bf16(S)
        qkAll = [Buf(), Buf()]
        V = v3(sb.bf16(S), 128)
        nkt = S // 128
        VB = bufs(nkt // 4)
        OS3 = v3(sb.f32(2 * S), S)
        OSB = Buf()
        oout = sb.bf16(S)
        ooutB = Buf()
        sqr = Ring([(sb.bf16(TT), Buf()) for _ in range(2)])
        rawr = Ring([(sb.f32(TT), Buf()) for _ in range(4)])
        rr = Ring([(sb.f32(TT), Buf()) for _ in range(3)])
        exr = Ring([(sb.bf16(256), Buf()) for _ in range(4)])
        ptr_ = Ring([(sb.bf16(256), Buf()) for _ in range(4)])
        vtr = Ring([(sb.bf16(TT), Buf()) for _ in range(2)])
        pvb = ps[:, 2, 0:256].bitcast(BF16)
        V2 = V.rearrange("p a b -> p (a b)")
        LA = 3
        pq = Ring([(ps[:, b, :], psB[b]) for b in (0, 1)])
        pss = (ps[:, 2, :], psB[2])
        pv = (ps[:, 3, :], psB[3])
        pst = Ring([(ps[:, b, 0:256], psB[b]) for b in (4, 5, 0, 1)])
        po = Ring([(ps[:, b, 0:256], psB[b]) for b in (6, 7)])

        for h in range(8):
            for g, (win, dil) in enumerate(GROUPS):
                gh = g * 8 + h
                L = S // dil
                nt = L // 128
                w, wB = wr.next()
                for t in range(3):
                    c0 = t * 3072 + g * 1024 + h * 128
                    fw.dma("sp", "aw%d" % (wr.i % 2), w[:, t, :, :],
                           wb_qkv[j, :, c0:c0 + 128].rearrange("(kc p) c -> p kc c", p=128), writes=[wB])
                K2 = 2 * ntt
                st_ = {}
                for k in range(K2 + 3):
                    if k < K2:
                        ti, t = k // 2, k % 2
                        pb, pbB = pq.next()
                        for kc in range(8):
                            mm(pb, w[:, t, kc, :], hT[:, kc, ti * TT:(ti + 1) * TT], kc == 0, kc == 7,
                               [wB, hTall], [pbB], kc == 7)
                        sq, sqB = sqr.next()
                        raw, rawB = rawr.next()
                        act(sq, pb, AF.Square, [pbB], [sqB])
                        act(raw, pb, AF.Identity, [pbB, gqB], [rawB], scale=gq[:, t:t + 1])
                        st_[k] = [sq, sqB, raw, rawB, None, None]
                    if 1 <= k <= K2:
                        e = st_[k - 1]
                        r_, rB_ = rr.next()
                        e[4], e[5] = r_, rB_
                        mm(pss[0], ones_b, e[0], True, True, [e[1], constB], [pss[1]], True)
                        act(r_, pss[0], AF.Ln, [pss[1], constB], [rB_], bias=epsc[:, 0:1], scale=1.0 / 128.0)
                    if 2 <= k <= K2 + 1:
                        e = st_[k - 2]
                        act(e[4], e[4], AF.Exp, [e[5]], [e[5]], scale=-0.5)
                    if 3 <= k <= K2 + 2:
                        kk = k - 3
                        e = st_.pop(kk)
                        ti, t = kk // 2, kk % 2
                        dst = (qT, kT)[t]
                        if dil == 1:
                            tt("pool", dst[:, ti * TT:(ti + 1) * TT], e[2], e[4], ALU.mult,
                               [e[3], e[5]], [qkAll[t]], relaxed=True)
                        else:
                            m0 = (ti * TT) // dil
                            dv = dst.rearrange("p (r m) -> p m r", r=dil)[:, m0:m0 + TT // dil, :]
                            tt("pool", dv, e[2].rearrange("p (m r) -> p m r", r=dil),
                               e[4].rearrange("p (m r) -> p m r", r=dil), ALU.mult,
                               [e[3], e[5]], [qkAll[t]], relaxed=True)
                tiles = [(r, i) for r in range(dil) for i in range(nt)]
                for v0 in range(0, nkt, 4):
                    if dil == 16:
                        nr = 4 // nt
                        r0 = tiles[v0][0]
                        for ri in range(nr):
                            for kc in range(8):
                                hv = hT[:, kc, :].rearrange("p (m r) -> p r m", r=16)[:, r0 + ri, :]
                                mm(pv[0][:, ri * L:(ri + 1) * L], w[:, 2, kc, :], hv, kc == 0, kc == 7,
                                   [wB, hTall], [pv[1]], (ri == nr - 1 and kc == 7))
                        vt, vtB = vtr.next()
                        cp("act", vt, pv[0], [pv[1]], [vtB])
                        for u in range(4):
                            tr(pvb[:, u * 128:(u + 1) * 128], vt[:, u * 128:(u + 1) * 128], identb_p,
                               [vtB, identbpB], [pss[1]], u == 3)
                        cp("dve", V2[:, v0 * 128:(v0 + 4) * 128], pvb, [pss[1]], [VB[v0 // 4]])
                        continue
                    for u in range(4):
                        r, i = tiles[v0 + u]
                        a0 = r + dil * 128 * i
                        for kc in range(8):
                            mm(pv[0][:, u * 128:(u + 1) * 128], hT[:, kc, a0:a0 + dil * 127 + 1:dil], w[:, 2, kc, :],
                               kc == 0, kc == 7, [wB, hTall], [pv[1]], (u == 3 and kc == 7))
                    cp("act", V[:, v0:v0 + 4, :], v3(pv[0], 128), [pv[1]], [VB[v0 // 4]])

                def qk(n):
                    r, i = tiles[n]
                    qlo = max(0, 128 * i - 64)
                    qhi = min(L, 128 * i + 192)
                    nq = qhi - qlo
                    sp_, spB = pst.next()
                    k0 = r * L + 128 * i
                    q0 = r * L + qlo
                    mm(sp_[:, 0:nq], kT[:, k0:k0 + 128], qT[:, q0:q0 + nq], True, True, qkAll, [spB], True)
                    return (sp_, spB, qlo, nq)

                pend = {}
                for n in range(min(LA, len(tiles))):
                    pend[n] = qk(n)
                prev = None
                for n in range(len(tiles)):
                    if n + LA < len(tiles):
                        pend[n + LA] = qk(n + LA)
                    r, i = tiles[n]
                    sp_, spB, qlo, nq = pend.pop(n)
                    off = qlo - (128 * i - 64)
                    ex, exB = exr.next()
                    pt, ptB = ptr_.next()
                    act(ex[:, 0:nq], sp_[:, 0:nq], AF.Exp, [spB], [exB])
                    tt("pool", pt[:, 0:nq], ex[:, 0:nq], E[:, gh, off:off + nq], ALU.mult, [exB, EB], [ptB])
                    if i == 0:
                        prev = None
                    cur = (n, pt, ptB, qlo)
                    done = [i] + ([nt] if i == nt - 1 else [])
                    for reg in done:
                        qa, qb = max(0, 128 * reg - 64), min(L, 128 * reg + 64)
                        wd = qb - qa
                        contrib = []
                        if reg >= 1:
                            pn, ppt, pptB, pqlo = prev if reg == i else cur
                            contrib.append((pn, ppt, pptB, qa - pqlo))
                        if reg <= nt - 1:
                            contrib.append((n, pt, ptB, qa - qlo))
                        pr, prB = po.next()
                        for ci, (vn, cpt, cptB, c0) in enumerate(contrib):
                            mm(pr[:, 0:wd], V[:, vn, :], cpt[:, c0:c0 + wd], ci == 0, ci == len(contrib) - 1,
                               [VB[vn // 4], cptB], [prB], False)
                        for ci, (vn, cpt, cptB, c0) in enumerate(contrib):
                            mm(pr[:, 128:128 + wd], ones_b, cpt[:, c0:c0 + wd], ci == 0, ci == len(contrib) - 1,
                               [cptB, constB], [prB], ci == len(contrib) - 1)
                        p0 = r + dil * qa
                        osl = OS3[:, :, p0:p0 + dil * (wd - 1) + 1:dil]
                        pr3 = v3(pr, 128)[:, :, 0:wd]
                        if g == 0:
                            cp("dve", osl, pr3, [prB], [OSB], relaxed=True)
                        else:
                            tt("dve", osl, pr3, osl, ALU.add, [prB, OSB], [OSB], relaxed=True)
                    prev = cur
            for c0 in range(0, S, 2048):
                act(OS3[:, 1, c0:c0 + 2048], OS3[:, 1, c0:c0 + 2048], AF.Ln, [OSB], [OSB])
                act(OS3[:, 1, c0:c0 + 2048], OS3[:, 1, c0:c0 + 2048], AF.Exp, [OSB], [OSB], scale=-1.0)
                tt("pool", oout[:, c0:c0 + 2048], OS3[:, 0, c0:c0 + 2048], OS3[:, 1, c0:c0 + 2048], ALU.mult,
                   [OSB], [ooutB])
            fw.dma("pool", "sto", oT_s[:, h, tb:tb + S], oout, reads=[ooutB])
        if l == 0 and si == 0:
            dbg_dump("d_qT", qT, S, BF16, [qkAll[0]])
            dbg_dump("d_kT", kT, S, BF16, [qkAll[1]])
            dbg_dump("d_V", V.rearrange("p a b -> p (a b)"), S, BF16, VB)
            dbg_dump("d_E", E.rearrange("p a b -> p (a b)"), 24 * 256, BF16, [EB])
            dbg_dump("d_O", OS3[:, 0, :], S, F32, [OSB])
            dbg_dump("d_s", OS3[:, 1, :], S, F32, [OSB])
            dbg_dump("d_oo", oout, S, BF16, [ooutB])
        sb.pop()

    def phase_ta(l):
        j = l // 2
        is_attn = (l % 2 == 0)
        sb.push()
        wproj = v3(sb.bf16(8 * D), D)
        win = v3(sb.bf16(8 * 2 * DFF), 2 * DFF)
        WB = Buf()
        src = wb_o if is_attn else wb_pw2
        fw.dma("sp", "wta", wproj, src[j].rearrange("(kc p) c -> p kc c", p=128), writes=[WB])
        for cb in range(11):
            fw.dma("sp", "wta", win[:, :, cb * 512:(cb + 1) * 512],
                   wb_in[l, :, cb * 512:(cb + 1) * 512].rearrange("(kc p) c -> p kc c", p=128), writes=[WB])
        inT = [(v3(sb.bf16(8 * TT), TT), Buf()) for _ in range(2)]
        inr = Ring(inT)
        xt = v3(sb.f32(8 * TT), TT)
        xtB = bufs(8)
        h2 = v3(sb.bf16(8 * TT), TT)
        h2B = bufs(8)
        aT = v3(sb.bf16(22 * TT), TT)
        aTB = bufs(22)
        gs = sb.f32(8)
        gb = sb.f32(8)
        modB = Buf()
        tmp = Ring([(sb.f32(TT), Buf()) for _ in range(2)])
        sqr = Ring([(sb.bf16(TT), Buf()) for _ in range(2)])
        sgr = Ring([(sb.f32(TT), Buf()) for _ in range(2)])
        r_ap, rB = sb.f32(TT), Buf()
        pbank = Ring([(ps[:, b, :], psB[b]) for b in range(0, 6)])
        for si in range(NSEQ):
            S = seqs[si][2]
            tb = tokbase[si]
            mod_gs(gs, ada(l, si, 4), col("n2g", l * 8, 8), modB)
            sh = ada(l, si, 3)
            g1 = ada(l, si, 2)
            if not is_attn:
                tt("dve", gb, g1, col("bpw2", j * 8, 8), ALU.mult, [adaB, coltabB], [modB])
            for ti in range(S // TT):
                t0 = tb + ti * TT
                it, itB = inr.next()
                fw.dma("sp", "tain%d" % (inr.i % 2), it, oT_s[:, :, t0:t0 + TT], writes=[itB])
                fw.dma("sp", "tax", xt, xT_s[:, :, t0:t0 + TT], writes=xtB)
                for oc in range(8):
                    pb, pbB = pbank.next()
                    for kc in range(8):
                        mm(pb, wproj[:, kc, oc * 128:(oc + 1) * 128], it[:, kc, :], kc == 0, kc == 7, [WB, itB],
                           [pbB], kc == 7)
                    if is_attn:
                        stt("dve", xt[:, oc, :], pb, g1[:, oc:oc + 1], xt[:, oc, :], ALU.mult, ALU.add,
                            [pbB, adaB, xtB[oc]], [xtB[oc]])
                    else:
                        tp, tpB = tmp.next()
                        act(tp, pb, AF.Identity, [pbB, adaB, modB], [tpB], bias=gb[:, oc:oc + 1],
                            scale=g1[:, oc:oc + 1])
                        tt("dve", xt[:, oc, :], xt[:, oc, :], tp, ALU.add, [tpB, xtB[oc]], [xtB[oc]])
                fw.dma("pool", "stx", xT_s[:, :, t0:t0 + TT], xt, reads=xtB)
                norm_mod(xt, xtB, h2, h2B, gs, sh, modB, tmp, sqr, (ps[:, 6, :], psB[6]), r_ap, rB)
                for fc in range(22):
                    pg, pgB = pbank.next()
                    pu, puB = pbank.next()
                    for kc in range(8):
                        mm(pg, win[:, kc, fc * 128:(fc + 1) * 128], h2[:, kc, :], kc == 0, kc == 7, [WB, h2B[kc]],
                           [pgB], kc == 7)
                    for kc in range(8):
                        mm(pu, win[:, kc, DFF + fc * 128:DFF + (fc + 1) * 128], h2[:, kc, :], kc == 0, kc == 7,
                           [WB, h2B[kc]], [puB], kc == 7)
                    sg, sgB = sgr.next()
                    act(sg, pg, AF.Silu, [pgB], [sgB])
                    tt("dve", aT[:, fc, :], pu, sg, ALU.mult, [puB, sgB], [aTB[fc]])
                fw.dma("pool", "sta0", aT_s[:, 0:11, t0:t0 + TT], aT[:, 0:11, :], reads=aTB[0:11])
                fw.dma("pool", "sta1", aT_s[:, 11:22, t0:t0 + TT], aT[:, 11:22, :], reads=aTB[11:22])
        sb.pop()

    def phase_tb(l):
        last = (l == NL - 1)
        nxt_conv = (not last) and ((l + 1) % 2 == 1)
        jn = (l + 1) // 2
        sb.push()
        wout = v3(sb.bf16(22 * D), D)
        WB = Buf()
        for half in range(2):
            fw.dma("sp", "wtb", wout[:, half * 11:(half + 1) * 11, :],
                   wb_out[l, half * 11 * 128:(half + 1) * 11 * 128, :].rearrange("(kc p) c -> p kc c", p=128),
                   writes=[WB])
        if nxt_conv:
            wpw1 = v3(sb.bf16(8 * 2 * D), 2 * D)
            for half in range(2):
                fw.dma("sp", "wtb", wpw1[:, :, half * D:(half + 1) * D],
                       wb_pw1[jn, :, half * D:(half + 1) * D].rearrange("(kc p) c -> p kc c", p=128), writes=[WB])
        aTt = [(v3(sb.bf16(22 * TT), TT), Buf()) for _ in range(2)]
        ar = Ring(aTt)
        xt = v3(sb.f32(8 * TT), TT)
        xtB = bufs(8)
        modB = Buf()
        tmp = Ring([(sb.f32(TT), Buf()) for _ in range(2)])
        pbank = Ring([(ps[:, b, :], psB[b]) for b in range(0, 4)])
        if last:
            yr = Ring([(sb.f32(D), Buf()) for _ in range(2)])
        else:
            hn = v3(sb.bf16(8 * TT), TT)
            hnB = bufs(8)
            gs = sb.f32(8)
            sqr = Ring([(sb.bf16(TT), Buf()) for _ in range(2)])
            r_ap, rB = sb.f32(TT), Buf()
            if nxt_conv:
                uT = v3(sb.bf16(8 * TT), TT)
                uTB = bufs(8)
                sgr = Ring([(sb.f32(TT), Buf()) for _ in range(2)])
        for si in range(NSEQ):
            S = seqs[si][2]
            tb = tokbase[si]
            g2 = ada(l, si, 5)
            if not last:
                mod_gs(gs, ada(l + 1, si, 1), col("n1g", (l + 1) * 8, 8), modB)
                sh = ada(l + 1, si, 0)
            for ti in range(S // TT):
                t0 = tb + ti * TT
                at, atB = ar.next()
                fw.dma("sp", "tbin%d" % (ar.i % 2), at[:, 0:11, :], aT_s[:, 0:11, t0:t0 + TT], writes=[atB])
                fw.dma("sp", "tbin%d" % (ar.i % 2), at[:, 11:22, :], aT_s[:, 11:22, t0:t0 + TT], writes=[atB])
                fw.dma("sp", "tbx", xt, xT_s[:, :, t0:t0 + TT], writes=xtB)
                for oc in range(8):
                    pb, pbB = pbank.next()
                    for fc in range(22):
                        mm(pb, wout[:, fc, oc * 128:(oc + 1) * 128], at[:, fc, :], fc == 0, fc == 21, [WB, atB],
                           [pbB], fc == 21)
                    stt("dve", xt[:, oc, :], pb, g2[:, oc:oc + 1], xt[:, oc, :], ALU.mult, ALU.add,
                        [pbB, adaB, xtB[oc]], [xtB[oc]])
                if last:
                    for s4 in range(4):
                        yt, ytB = yr.next()
                        for hf in range(2):
                            pb, pbB = (ps[:, 4 + hf, :], psB[4 + hf])
                            for c4 in range(4):
                                c = hf * 4 + c4
                                tr(pb[:, c4 * 128:(c4 + 1) * 128], xt[:, c, s4 * 128:(s4 + 1) * 128], ident,
                                   [xtB[c], identB], [pbB], c4 == 3)
                            cp("act" if hf else "dve", yt[:, hf * 512:(hf + 1) * 512], pb, [pbB], [ytB])
                        fw.dma("pool", "sty%d" % (yr.i % 2), yrows(si, ti * TT + s4 * 128, 128), yt, reads=[ytB])
                else:
                    fw.dma("pool", "stx", xT_s[:, :, t0:t0 + TT], xt, reads=xtB)
                    norm_mod(xt, xtB, hn, hnB, gs, sh, modB, tmp, sqr, (ps[:, 6, :], psB[6]), r_ap, rB)
                    if not nxt_conv:
                        fw.dma("pool", "sth", hT_s[:, :, t0:t0 + TT], hn, reads=hnB)
                    else:
                        for cc in range(8):
                            pa, paB = pbank.next()
                            pg, pgB = pbank.next()
                            for kc in range(8):
                                mm(pa, wpw1[:, kc, cc * 128:(cc + 1) * 128], hn[:, kc, :], kc == 0, kc == 7,
                                   [WB, hnB[kc]], [paB], kc == 7)
                            for kc in range(8):
                                mm(pg, wpw1[:, kc, D + cc * 128:D + (cc + 1) * 128], hn[:, kc, :], kc == 0, kc == 7,
                                   [WB, hnB[kc]], [pgB], kc == 7)
                            sg, sgB = sgr.next()
                            act(sg, pg, AF.Sigmoid, [pgB, coltabB], [sgB], bias=col("bpw1", jn * 16 + 8 + cc),
                                scale=1.0)
                            stt("dve", uT[:, cc, :], pa, col("bpw1", jn * 16 + cc), sg, ALU.add, ALU.mult,
                                [paB, coltabB, sgB], [uTB[cc]])
                        fw.dma("pool", "stu", uT_s[:, :, t0:t0 + TT], uT, reads=uTB)
        sb.pop()

    def phase_conv(l):
        j = l // 2
        PAD = (CONVW - 1) // 2
        HW = TT + 2 * PAD
        sb.push()
        dg = sb.bf16(8 * CONVW * 128).rearrange("p (c t k) -> p c t k", c=8, t=CONVW)
        DGB = Buf()
        for c in range(8):
            fw.dma("sp", "dgl", dg[:, c, :, :],
                   dg_s[j, :, c * CONVW * 128:(c + 1) * CONVW * 128].rearrange("p (t k) -> p t k", k=128),
                   writes=[DGB])
        ut = [(v3(sb.bf16(8 * HW), HW), Buf()) for _ in range(2)]
        ur = Ring(ut)
        v = v3(sb.f32(8 * TT), TT)
        vB = bufs(8)
        vb = v3(sb.bf16(8 * TT), TT)
        vbB = bufs(8)
        sq = v3(sb.bf16(8 * TT), TT)
        sqB = bufs(8)
        z = v3(sb.bf16(8 * TT), TT)
        zB = bufs(8)
        mean, msq, var = sb.f32(TT), sb.f32(TT), sb.f32(TT)
        stB = Buf()
        t1r = Ring([(sb.f32(TT), Buf()) for _ in range(2)])
        t2r = Ring([(sb.f32(TT), Buf()) for _ in range(2)])
        pbank = Ring([(ps[:, b, :], psB[b]) for b in range(0, 4)])
        psum_, psumB = ps[:, 4, :], psB[4]
        psq_, psqB = ps[:, 5, :], psB[5]
        for si in range(NSEQ):
            S = seqs[si][2]
            tb = tokbase[si]
            nt = S // TT
            for ti in range(nt):
                t0 = ti * TT
                u, uB = ur.next()
                lo = max(0, t0 - PAD)
                hi = min(S, t0 + TT + PAD)
                if ti == 0:
                    memset("pool", u[:, :, 0:PAD], 0.0, [uB])
                if ti == nt - 1:
                    memset("pool", u[:, :, HW - PAD:HW], 0.0, [uB])
                fw.dma("sp", "cvin%d" % (ur.i % 2), u[:, :, lo - (t0 - PAD):hi - (t0 - PAD)],
                       uT_s[:, :, tb + lo:tb + hi], writes=[uB])

                def stats_mm(c):
                    mm(psum_, ones_b, vb[:, c, :], c == 0, c == 7, [vbB[c], constB], [psumB], True)
                    mm(psq_, ones_b, sq[:, c, :], c == 0, c == 7, [sqB[c], constB], [psqB], True)

                for c in range(8):
                    pb, pbB = pbank.next()
                    for t in range(CONVW):
                        mm(pb, dg[:, c, t, :], u[:, c, t:t + TT], t == 0, t == CONVW - 1, [DGB, uB], [pbB],
                           t == CONVW - 1)
                    if c >= 1:
                        stats_mm(c - 1)
                    bcol = col("bdw", j * 8 + c)
                    act(v[:, c, :], pb, AF.Identity, [pbB, coltabB], [vB[c]], bias=bcol, scale=1.0)
                    act(sq[:, c, :], pb, AF.Square, [pbB, coltabB], [sqB[c]], bias=bcol, scale=1.0)
                    cp("dve", vb[:, c, :], v[:, c, :], [vB[c]], [vbB[c]])
                stats_mm(7)
                ts("dve", mean, psum_, 1.0 / D, None, ALU.mult, None, [psumB], [stB])
                tt("dve", msq, mean, mean, ALU.mult, [stB], [stB])
                stt("dve", var, psq_, 1.0 / D, msq, ALU.mult, ALU.subtract, [psqB, stB], [stB])
                act(var, var, AF.Sqrt, [stB, constB], [stB], bias=epsc[:, 0:1], scale=1.0)
                recip(var, var, [stB], [stB])
                for c in range(8):
                    t1, t1B = t1r.next()
                    t2, t2B = t2r.next()
                    tt("pool", t1, v[:, c, :], mean, ALU.subtract, [vB[c], stB], [t1B])
                    tt("dve", t2, t1, var, ALU.mult, [t1B, stB], [t2B])
                    act(z[:, c, :], t2, AF.Silu, [t2B, coltabB], [zB[c]], bias=col("lnb", j * 8 + c),
                        scale=col("lng", j * 8 + c))
                fw.dma("pool", "stz", oT_s[:, :, tb + t0:tb + t0 + TT], z, reads=zB)
        sb.pop()

    phases = [(phase_in, ())]
    for l in range(NL):
        phases.append((phase_att if l % 2 == 0 else phase_conv, (l,)))
        phases.append((phase_ta, (l,)))
        phases.append((phase_tb, (l,)))
    for k, (fn_, args_) in enumerate(phases):
        if debug is not None and k >= debug:
            break
        fn_(*args_)
    build_program.stats = dict(sb_peak=sb.peak, n_instr={k: len(e.q) for k, e in fw.E.items()}, nsem=len(fw.sems))

    fw.barrier()
    fw.run()
    return nc


_CACHE = {}


def kernel(x_prompt, x_sample, c_prompt, c_sample, rel_bias_table, norm1_g, norm2_g,
           ada_w, ada_b, attn_w_qkv, attn_q_gain, attn_k_gain, attn_w_o,
           conv_w_pw1, conv_b_pw1, conv_w_dw, conv_b_dw, conv_ln_g, conv_ln_b,
           conv_w_pw2, conv_b_pw2, ffn_w_in, ffn_w_out):
    NC = 8
    f = lambda a: np.ascontiguousarray(np.asarray(a, dtype=np.float32))
    x_prompt, x_sample, c_prompt, c_sample = f(x_prompt), f(x_sample), f(c_prompt), f(c_sample)
    BP, SP_, _ = x_prompt.shape
    BS, SS_, _ = x_sample.shape
    n_p, n_s = BP // NC, BS // NC
    key = (n_p, n_s, SP_, SS_)
    if key not in _CACHE:
        _CACHE[key] = build_program(n_p, n_s, SP_, SS_)
    nc = _CACHE[key]
    shared = {
        "rel_bias_table": f(rel_bias_table), "norm1_g": f(norm1_g), "norm2_g": f(norm2_g),
        "ada_w": f(ada_w), "ada_b": f(ada_b), "attn_w_qkv": f(attn_w_qkv),
        "attn_q_gain": f(attn_q_gain), "attn_k_gain": f(attn_k_gain), "attn_w_o": f(attn_w_o),
        "conv_w_pw1": f(conv_w_pw1), "conv_b_pw1": f(conv_b_pw1), "conv_w_dw": f(conv_w_dw),
        "conv_b_dw": f(conv_b_dw), "conv_ln_g": f(conv_ln_g), "conv_ln_b": f(conv_ln_b),
        "conv_w_pw2": f(conv_w_pw2), "conv_b_pw2": f(conv_b_pw2), "ffn_w_in": f(ffn_w_in),
        "ffn_w_out": f(ffn_w_out),
        "ident": np.eye(128, dtype=np.float32), "onehot": make_onehot(),
    }
    in_maps = []
    for c in range(NC):
        m = dict(shared)
        m["xp"] = x_prompt[c * n_p:(c + 1) * n_p]
        m["xs"] = x_sample[c * n_s:(c + 1) * n_s]
        m["cp"] = c_prompt[c * n_p:(c + 1) * n_p]
        m["cs"] = c_sample[c * n_s:(c + 1) * n_s]
        in_maps.append(m)
    res = run_bass_kernel_spmd(nc, in_maps, core_ids=list(range(NC)))
    yp = np.concatenate([np.asarray(r["yp"]) for r in res.results], axis=0).astype(np.float32)
    ys = np.concatenate([np.asarray(r["ys"]) for r in res.results], axis=0).astype(np.float32)
    return (yp, ys)
```
